# Optimizing a Trainium2 kernel written in Bass

```python
import jax, jax.numpy as jnp
from jax import lax
import numpy as np

D_MODEL = 1024
BATCH = 8
SEQ = 2048
DEPTH = 1

HEAD_DIM = 64
N_SLOT_HEADS = 8
DILATED_PATTERNS = ((128, 1), (512, 4), (2048, 16))
N_GROUPS = len(DILATED_PATTERNS)
ATTN_HEADS = N_GROUPS * N_SLOT_HEADS
ATTN_QKV_WIDTH = ATTN_HEADS * HEAD_DIM
ATTN_OUT_WIDTH = N_SLOT_HEADS * HEAD_DIM
BLK = max(w // (2 * d) for (w, d) in DILATED_PATTERNS)
ROPE_THETA = 500000.0
ROT_DIM = HEAD_DIM // 4
CONV_CH = D_MODEL // 2
CONV_WIDTH = 31
N_BRANCH = 2
D_FF = -(-8 * D_MODEL // 768) * 256
IN_WIDTH = 3 * ATTN_QKV_WIDTH + 2 * CONV_CH + N_BRANCH * D_MODEL
EPS = 1e-6
NEG_INF = -1e30

kernel_name = 'hybrid_dilated_attn_conformer_conv_block'


def rmsnorm(t, w):
    tf = t.astype(jnp.float32)
    y = tf * lax.rsqrt(jnp.mean(tf * tf, axis=-1, keepdims=True) + EPS)
    return (y * w.astype(jnp.float32)).astype(t.dtype)


def layernorm(t, w, b):
    tf = t.astype(jnp.float32)
    mu = jnp.mean(tf, axis=-1, keepdims=True)
    var = jnp.mean(jnp.square(tf - mu), axis=-1, keepdims=True)
    y = (tf - mu) * lax.rsqrt(var + EPS)
    return (y * w.astype(jnp.float32) + b.astype(jnp.float32)).astype(t.dtype)


def partial_rope(t, cos, sin):
    tf = t.astype(jnp.float32)
    half = ROT_DIM // 2
    t1, t2, rest = tf[..., :half], tf[..., half:ROT_DIM], tf[..., ROT_DIM:]
    rot = jnp.concatenate([t1 * cos - t2 * sin, t2 * cos + t1 * sin, rest], axis=-1)
    return rot.astype(t.dtype)


def dilated_window_attention(q, k, v, dilation, half_span):
    B, S, H, Dh = q.shape
    L = S // dilation
    nb = -(-L // BLK)
    Lp = nb * BLK

    def residue_major(t):
        return t.reshape(B, L, dilation, H, Dh).transpose(0, 2, 3, 1, 4)

    qr, kr, vr = residue_major(q), residue_major(k), residue_major(v)
    qb = jnp.pad(qr, [(0, 0)] * 3 + [(0, Lp - L), (0, 0)]).reshape(B, dilation, H, nb, BLK, Dh)

    def banded(t):
        tb = jnp.pad(t, [(0, 0)] * 3 + [(BLK, Lp - L + BLK), (0, 0)])
        tb = tb.reshape(B, dilation, H, nb + 2, BLK, Dh)
        return jnp.concatenate([tb[:, :, :, j:j + nb] for j in range(3)], axis=4)

    kb, vb = banded(kr), banded(vr)
    qpos = jnp.arange(nb)[:, None] * BLK + jnp.arange(BLK)[None, :]
    kpos = jnp.arange(nb)[:, None] * BLK - BLK + jnp.arange(3 * BLK)[None, :]
    dist = jnp.abs(qpos[:, :, None] - kpos[:, None, :])
    valid = (dist <= half_span) & (kpos[:, None, :] >= 0) & (kpos[:, None, :] < L)

    s = jnp.einsum('bdhnqc,bdhnkc->bdhnqk', qb.astype(jnp.float32), kb.astype(jnp.float32))
    s = jnp.where(valid, s * (HEAD_DIM ** -0.5), NEG_INF)
    m = jnp.max(s, axis=-1, keepdims=True)
    p = jnp.exp(s - m)
    den = jnp.sum(p, axis=-1, keepdims=True)
    o = jnp.einsum('bdhnqk,bdhnkc->bdhnqc', p, vb.astype(jnp.float32)) / den
    lse = (m + jnp.log(den))[..., 0]
    o = o.reshape(B, dilation, H, Lp, Dh)[:, :, :, :L].transpose(0, 3, 1, 2, 4).reshape(B, S, H, Dh)
    lse = lse.reshape(B, dilation, H, Lp)[..., :L].transpose(0, 3, 1, 2).reshape(B, S, H)
    return o, lse


def depthwise_conv(u, w, b):
    pad = (CONV_WIDTH - 1) // 2
    y = lax.conv_general_dilated(
        u, w[:, None, :].astype(u.dtype), window_strides=(1,), padding=[(pad, pad)],
        dimension_numbers=('NWC', 'WIO', 'NWC'), feature_group_count=CONV_CH)
    return y + b.astype(u.dtype)


def setup_inputs(seed: int = 0) -> dict:
    key = jax.random.key(seed)
    ks = jax.random.split(key, 20)
    f32 = jnp.float32

    def nrm(k, shape, scale):
        return jax.random.normal(k, shape, f32) * scale

    x = jax.random.normal(ks[0], (BATCH, SEQ, D_MODEL), f32)
    offsets = jax.random.randint(ks[1], (BATCH, 1), 0, 4096, dtype=jnp.int32)
    positions = (offsets + jnp.arange(SEQ, dtype=jnp.int32)[None, :]).astype(jnp.int32)
    return {
        'x': x,
        'positions': positions,
        'norm1_w': 1.0 + nrm(ks[2], (DEPTH, D_MODEL), 0.02),
        'w_in': nrm(ks[3], (DEPTH, D_MODEL, IN_WIDTH), D_MODEL ** -0.5),
        'b_gate': nrm(ks[4], (DEPTH, N_BRANCH, D_MODEL), 0.02),
        'q_norm_w': 1.0 + nrm(ks[5], (DEPTH, HEAD_DIM), 0.02),
        'k_norm_w': 1.0 + nrm(ks[6], (DEPTH, HEAD_DIM), 0.02),
        'w_o_attn': nrm(ks[7], (DEPTH, ATTN_OUT_WIDTH, D_MODEL), ATTN_OUT_WIDTH ** -0.5),
        'conv_w': nrm(ks[8], (DEPTH, CONV_WIDTH, CONV_CH), CONV_WIDTH ** -0.5),
        'conv_b': nrm(ks[9], (DEPTH, CONV_CH), 0.02),
        'conv_ln_w': 1.0 + nrm(ks[10], (DEPTH, CONV_CH), 0.02),
        'conv_ln_b': nrm(ks[11], (DEPTH, CONV_CH), 0.02),
        'w_pw_conv': nrm(ks[12], (DEPTH, CONV_CH, D_MODEL), CONV_CH ** -0.5),
        'w_out': nrm(ks[13], (DEPTH, D_MODEL, D_MODEL), D_MODEL ** -0.5),
        'norm2_w': 1.0 + nrm(ks[14], (DEPTH, D_MODEL), 0.02),
        'w_ffn_in': nrm(ks[15], (DEPTH, D_MODEL, 2 * D_FF), D_MODEL ** -0.5),
        'w_ffn_out': nrm(ks[16], (DEPTH, D_FF, D_MODEL), D_FF ** -0.5),
    }


def reference(x, positions, norm1_w, w_in, b_gate, q_norm_w, k_norm_w, w_o_attn,
              conv_w, conv_b, conv_ln_w, conv_ln_b, w_pw_conv, w_out, norm2_w,
              w_ffn_in, w_ffn_out):
    B, S, _ = x.shape
    inv_freq = ROPE_THETA ** (-jnp.arange(0, ROT_DIM, 2, dtype=jnp.float32) / ROT_DIM)
    ang = positions.astype(jnp.float32)[..., None] * inv_freq
    cos = jnp.cos(ang)[:, :, None, None, :]
    sin = jnp.sin(ang)[:, :, None, None, :]
    split_at = [ATTN_QKV_WIDTH, 2 * ATTN_QKV_WIDTH, 3 * ATTN_QKV_WIDTH,
                3 * ATTN_QKV_WIDTH + 2 * CONV_CH]

    for l in range(DEPTH):
        h = rmsnorm(x, norm1_w[l])
        proj = h @ w_in[l].astype(h.dtype)
        q, k, v, conv_in, gate_logits = jnp.split(proj, split_at, axis=-1)
        hshape = (B, S, N_GROUPS, N_SLOT_HEADS, HEAD_DIM)
        q = partial_rope(rmsnorm(q.reshape(hshape), q_norm_w[l]), cos, sin)
        k = partial_rope(rmsnorm(k.reshape(hshape), k_norm_w[l]), cos, sin)
        v = v.reshape(hshape)

        outs, lses = [], []
        for g, (window, dilation) in enumerate(DILATED_PATTERNS):
            o_g, lse_g = dilated_window_attention(q[:, :, g], k[:, :, g], v[:, :, g],
                                                  dilation, window // (2 * dilation))
            outs.append(o_g)
            lses.append(lse_g)
        mix = jax.nn.softmax(jnp.stack(lses, axis=0), axis=0)
        attn = jnp.sum(mix[..., None] * jnp.stack(outs, axis=0), axis=0)
        attn = attn.reshape(B, S, ATTN_OUT_WIDTH).astype(x.dtype)
        y_a = attn @ w_o_attn[l].astype(x.dtype)

        a, b = jnp.split(conv_in, 2, axis=-1)
        u = a * jax.nn.sigmoid(b)
        u = depthwise_conv(u, conv_w[l], conv_b[l])
        u = jax.nn.silu(layernorm(u, conv_ln_w[l], conv_ln_b[l]))
        y_b = u @ w_pw_conv[l].astype(u.dtype)

        gates = jax.nn.sigmoid(gate_logits + b_gate[l].reshape(N_BRANCH * D_MODEL).astype(x.dtype))
        g_a, g_b = jnp.split(gates, 2, axis=-1)
        x = x + (g_a * y_a + g_b * y_b) @ w_out[l].astype(x.dtype)

        h2 = rmsnorm(x, norm2_w[l])
        gt, up = jnp.split(h2 @ w_ffn_in[l].astype(h2.dtype), 2, axis=-1)
        x = x + (jax.nn.silu(gt) * up) @ w_ffn_out[l].astype(x.dtype)
    return x
```

```python
import numpy as np
import concourse.bass as bass
import concourse.mybir as mybir
from concourse.bass_utils import run_bass_kernel_spmd

F32 = mybir.dt.float32
BF16 = mybir.dt.bfloat16
I32 = mybir.dt.int32
ALU = mybir.AluOpType
AF = mybir.ActivationFunctionType

S = 2048
D = 1024
NT = 4
TB = 512
EPS = 1e-6
NEG = -30000.0
PI = float(np.pi)

V_N1W, V_N2W, V_QW, V_KW, V_BGA, V_BGB, V_CB, V_LW, V_LB, V_INVF, V_CW, V_SELB = 0, 8, 16, 17, 18, 26, 34, 38, 42, 46, 48, 172
NV = 172 + 512
M_ID, M_BD, M_ONES, M_ROT, M_MID, M_FIRST, M_LAST, M_G2, M_SEL = 0, 128, 256, 384, 512, 1024, 1280, 1536, 1792
NCM = 1792 + 64

A_ACC = 0
A_DEN = 8192
A_FS = 10240
A_SP = 14336
A_X1 = 0
A_HT = 16384
A_R2 = 24576
A_R3 = 33280
A_W = 41472
A_SB = 47616
A_END = 49664


class Op:
    __slots__ = ("eng", "fn", "deps", "signal", "val", "is_dma", "sem_key", "waits", "idx")


class Sched:
    def __init__(self):
        self.ops = []
        self.last_w = {}
        self.readers = {}
        self.barrier_op = None

    def add(self, eng, fn, reads=(), writes=(), dma_sem=None):
        op = Op()
        op.eng = eng
        op.fn = fn
        op.signal = False
        op.val = 0
        op.is_dma = dma_sem is not None
        op.sem_key = dma_sem
        op.idx = len(self.ops)
        writes = list(writes)
        if ("ps", 2) in writes:
            writes += [("psden", 0), ("psden", 1)]
        elif ("psden", 0) in writes or ("psden", 1) in writes:
            writes += [("ps", 2)]
        deps = {}
        if self.barrier_op is not None:
            deps[self.barrier_op.idx] = self.barrier_op
        for k in reads:
            w = self.last_w.get(k)
            if w is not None:
                deps[w.idx] = w
        for k in writes:
            w = self.last_w.get(k)
            if w is not None:
                deps[w.idx] = w
            for r in self.readers.get(k, {}).values():
                if isinstance(r, list):
                    for rr in r:
                        deps[rr.idx] = rr
                else:
                    deps[r.idx] = r
        out = []
        for d in deps.values():
            if d is op:
                continue
            if (not d.is_dma) and (not op.is_dma) and d.eng == "pe" and eng == "pe":
                continue
            out.append(d)
        op.deps = out
        for k in reads:
            rd = self.readers.setdefault(k, {})
            if op.is_dma:
                rd.setdefault("dma", []).append(op)
            else:
                rd[eng] = op
        for k in writes:
            self.last_w[k] = op
            self.readers[k] = {}
        self.ops.append(op)
        return op

    def finalize(self):
        for op in self.ops:
            for d in op.deps:
                d.signal = True
        cnt = {}
        for op in self.ops:
            if op.is_dma:
                cnt[op.sem_key] = cnt.get(op.sem_key, 0) + 16
                op.val = cnt[op.sem_key]
            elif op.signal:
                cnt[op.eng] = cnt.get(op.eng, 0) + 1
                op.val = cnt[op.eng]
        self.totals = cnt
        known = {}
        for op in self.ops:
            kn = known.setdefault(op.eng, {})
            w = {}
            for d in op.deps:
                sn = d.sem_key if d.is_dma else d.eng
                if d.val > w.get(sn, 0):
                    w[sn] = d.val
            op.waits = []
            for sn, v in w.items():
                if v > kn.get(sn, 0):
                    op.waits.append((sn, v))
                    kn[sn] = v


class _Stop(Exception):
    pass


def build_program(stop=None):
    nc = bass.Bass("TRN2", target_bir_lowering=False)
    dt = nc.dram_tensor
    xT = dt("xT", [D, S], F32, kind="ExternalInput").ap()
    posr = dt("posr", [128, S], I32, kind="ExternalInput").ap()
    vecs_d = dt("vecs", [128, NV], F32, kind="ExternalInput").ap()
    cm_d = dt("cmats", [128, NCM], F32, kind="ExternalInput").ap()
    w_in_c = dt("w_in_c", [60, 128, 1024], F32, kind="ExternalInput").ap()
    w_v = dt("w_v", [3, 128, 4096], F32, kind="ExternalInput").ap()
    w_oa = dt("w_oa", [8, 128, 512], F32, kind="ExternalInput").ap()
    w_pw = dt("w_pw", [8, 128, 512], F32, kind="ExternalInput").ap()
    w_out_c = dt("w_out_c", [8, 128, 1024], F32, kind="ExternalInput").ap()
    w_fi = dt("w_fi", [44, 128, 1024], F32, kind="ExternalInput").ap()
    w_fo = dt("w_fo", [22, 128, 1024], F32, kind="ExternalInput").ap()
    outT = dt("outT", [D, S], F32, kind="ExternalOutput").ap()

    from contextlib import ExitStack
    es = ExitStack()
    arena_t = es.enter_context(nc.sbuf_tensor("arena", [128, A_END], F32))
    vecs_t = es.enter_context(nc.sbuf_tensor("vecs_sb", [128, NV], F32))
    cm_t = es.enter_context(nc.sbuf_tensor("cm_sb", [128, NCM], BF16))
    cf_t = es.enter_context(nc.sbuf_tensor("cf_sb", [128, 8], F32))
    ps_t = es.enter_context(nc.psum_tensor("ps_all", [128, 4096], F32))
    arena = arena_t[:]
    vecs = vecs_t[:]
    cm = cm_t[:]
    cf = cf_t[:]
    ps_all = ps_t[:]
    psb = [ps_all[:, i * 512:(i + 1) * 512] for i in range(8)]

    def fv(off, n):
        return arena[:, off:off + n]

    def bv(off_words, n_elems):
        return arena[:, off_words:off_words + n_elems // 2].bitcast(BF16)

    acc = fv(A_ACC, 8192).rearrange("p (a t) -> p a t", a=4)
    den = fv(A_DEN, 2048)
    fs = fv(A_FS, 4096)
    sp_f = fv(A_SP, 2048)
    x1T = fv(A_X1, 16384).rearrange("p (a t) -> p a t", a=8)
    hT = bv(A_HT, 16384).rearrange("p (a t) -> p a t", a=8)
    qT = bv(A_R2, 8192).rearrange("p (a t) -> p a t", a=4)
    kT = bv(A_R2 + 4096, 8192).rearrange("p (a t) -> p a t", a=4)
    vtile = bv(A_R2 + 8192, 1024).rearrange("p (a t) -> p a t", a=2)
    UW = 2080
    uT = bv(A_R2, 4 * UW).rearrange("p (a t) -> p a t", a=4)
    diag01 = bv(A_R2 + 4160, 2 * 31 * 128).rearrange("p (c j m) -> p c j m", c=2, j=31)
    diag23 = bv(A_R3 + 4096, 2 * 31 * 128).rearrange("p (c j m) -> p c j m", c=2, j=31)
    zT = bv(A_R2, 16384).rearrange("p (a t) -> p a t", a=8)
    aT = bv(A_R2, 16384).rearrange("p (a t) -> p a t", a=8)
    ctab = fv(A_R3, 2048)
    stab = fv(A_R3 + 2048, 2048)
    xst1 = fv(A_R3 + 4096, 4096).rearrange("p (a t) -> p a t", a=2)
    ucT = bv(A_R3, 8192).rearrange("p (a t) -> p a t", a=4)
    attnT = bv(A_R3 + 4096, 8192).rearrange("p (a t) -> p a t", a=4)
    wsl = bv(A_W, 12288).rearrange("p (s k m) -> p s k m", s=12, k=8)
    wv_view = bv(A_W + 4096, 4096).rearrange("p (k n) -> p k n", k=8)
    sb = bv(A_SB, 4096)

    ident = cm[:, M_ID:M_ID + 128]
    bd_ones = cm[:, M_BD:M_BD + 128]
    ones_m = cm[:, M_ONES:M_ONES + 128]
    rotR = cm[:, M_ROT:M_ROT + 128]
    sel = cm[:, M_SEL:M_SEL + 64].rearrange("p (h j) -> p h j", h=8)

    sc = Sched()
    add = sc.add
    outT3 = outT.rearrange("(a p) t -> p a t", p=128)

    def chk(name, ap):
        if stop != name:
            return
        if len(ap.shape) == 2:
            dst = outT[0:ap.shape[0], 0:ap.shape[1]]
        else:
            dst = outT3[0:ap.shape[0], 0:ap.shape[1], 0:ap.shape[2]]
        keys = list(dict.fromkeys(list(sc.last_w.keys()) + list(sc.readers.keys())))
        add("pool", lambda e: e.dma_start(out=dst, in_=ap), reads=keys, writes=[("out", 0)], dma_sem="out")
        raise _Stop()

    try:
        _phases(locals())
    except _Stop:
        pass
    _emit(locals())
    es.close()
    return nc


def _phases(L):
    globals_ = L
    (nc, add, sc, chk, fv, bv, arena, vecs, cm, cf, psb, acc, den, fs, sp_f, x1T, hT, qT, kT, vtile, uT, diag01, diag23, zT, aT,
     ctab, stab, xst1, ucT, attnT, wsl, wv_view, sb, ident, bd_ones, ones_m, rotR, sel, xT, posr, vecs_d, cm_d, w_in_c, w_v,
     w_oa, w_pw, w_out_c, w_fi, w_fo, outT, ps_all) = [L[k] for k in (
        "nc", "add", "sc", "chk", "fv", "bv", "arena", "vecs", "cm", "cf", "psb", "acc", "den", "fs", "sp_f", "x1T", "hT", "qT", "kT",
        "vtile", "uT", "diag01", "diag23", "zT", "aT", "ctab", "stab", "xst1", "ucT", "attnT", "wsl", "wv_view", "sb", "ident",
        "bd_ones", "ones_m", "rotR", "sel", "xT", "posr", "vecs_d", "cm_d", "w_in_c", "w_v", "w_oa", "w_pw", "w_out_c", "w_fi", "w_fo",
        "outT", "ps_all")]

    add("sp", lambda e: e.dma_start(out=vecs, in_=vecs_d), writes=["vecs"], dma_sem="const")
    add("pool", lambda e: e.dma_start(out=cm, in_=cm_d), writes=["cm"], dma_sem="constb")
    add("pool", lambda e: e.memset(cf[:, 0:1], EPS), writes=["cf"])

    def load_w(slot, src, n=1024, after=()):
        dst = bv(A_W + slot * 512, n)
        add("pool", lambda e: e.dma_start(out=dst, in_=src), reads=list(after), writes=[("w", slot + i) for i in range(n // 1024)],
            dma_sem="w%d" % slot)

    def load_qkv(g, after=()):
        for hp in range(4):
            load_w(hp, w_in_c[g * 4 + hp], after=(after if hp > 0 else ()))
        for hp in range(4):
            load_w(4 + hp, w_in_c[12 + g * 4 + hp], after=after)
        load_w(8, w_v[g], n=4096, after=after)

    def barrier():
        keys = list(sc.last_w.keys()) + list(sc.readers.keys())
        keys = list(dict.fromkeys(keys))
        sc.barrier_op = None
        sc.barrier_op = add("pool", lambda e: e.nop(), reads=[], writes=keys)

    def rms_pass1(c, xap, xkey, bank0):
        sqb = sb[:, 0:4096].rearrange("p (a t) -> p a t", a=2)
        add("act", lambda e, xap=xap, c=c: e.activation(out=sqb[:, c % 2, :], in_=xap, func=AF.Square),
            reads=[xkey], writes=[("sq", c % 2)])
        for tb in range(NT):
            add("pe", lambda e, c=c, tb=tb: e.matmul(psb[bank0 + tb], lhsT=ones_m, rhs=sqb[:, c % 2, tb * TB:(tb + 1) * TB],
                                                     start=(c == 0), stop=(c == 7)),
                reads=[("sq", c % 2), "cm"], writes=[("ps", bank0 + tb)])

    def rms_rstd(rstd, bank0):
        for tb in range(NT):
            add("act", lambda e, tb=tb: e.activation(out=rstd[:, tb * TB:(tb + 1) * TB], in_=psb[bank0 + tb], func=AF.Ln,
                                                     bias=cf[:, 0:1], scale=1.0 / D),
                reads=[("ps", bank0 + tb), "cf"], writes=[("rstd", tb)])
            add("act", lambda e, tb=tb: e.activation(out=rstd[:, tb * TB:(tb + 1) * TB], in_=rstd[:, tb * TB:(tb + 1) * TB],
                                                     func=AF.Exp, scale=-0.5),
                reads=[("rstd", tb)], writes=[("rstd", tb)])

    def rms_pass2(c, xap, xkey, dst, wcol, tag, rstd):
        add("dve", lambda e, xap=xap, c=c: e.scalar_tensor_tensor(out=dst[:, c, :], in0=xap, scalar=vecs[:, wcol + c:wcol + c + 1],
                                                                  in1=rstd, op0=ALU.mult, op1=ALU.mult),
            reads=[xkey, "vecs"] + [("rstd", tb) for tb in range(NT)], writes=[(tag, c)])

    def rms_pass2_tb(xaps, xkeys, dst, wcol, tag, rstd):
        for tb in range(NT):
            tok = slice(tb * TB, (tb + 1) * TB)
            for c in range(8):
                add("dve", lambda e, c=c, tok=tok: e.scalar_tensor_tensor(out=dst[:, c, tok], in0=xaps[c][:, tok],
                                                                          scalar=vecs[:, wcol + c:wcol + c + 1], in1=rstd[:, tok],
                                                                          op0=ALU.mult, op1=ALU.mult),
                    reads=[xkeys[c], "vecs", ("rstd", tb)], writes=[(tag, c, tb)])

    def rms_phase(load_chunk, dst, wcol, stage_key, tag, rstd, bank0=0):
        for c in range(8):
            xap, xkey = load_chunk(c, 0)
            rms_pass1(c, xap, xkey, bank0)
        rms_rstd(rstd, bank0)
        for c in range(8):
            xap, xkey = load_chunk(c, 1)
            rms_pass2(c, xap, xkey, dst, wcol, tag, rstd)

    xall = [fv(A_ACC + c * 2048, 2048) if c < 4 else fv(A_R2 + (c - 4) * 2048, 2048) for c in range(8)]
    posi = fv(A_FS, 2048).bitcast(I32)
    angk = fv(A_FS + 2048, 2048)
    ang = sp_f
    for c in range(8):
        add("sp", lambda e, c=c: e.dma_start(out=xall[c], in_=xT[c * 128:(c + 1) * 128, :]),
            writes=[("xall", c)], dma_sem="xa%d" % c)
        if c == 1:
            add("sp", lambda e: e.dma_start(out=posi, in_=posr), writes=["fs0"], dma_sem="xst0")
    load_qkv(0, after=[("xall", 7)])
    for c in range(8):
        rms_pass1(c, xall[c], ("xall", c), 0)
    rstd1 = fv(A_DEN, 2048)
    rms_rstd(rstd1, 0)
    add("dve", lambda e: e.tensor_copy(out=angk, in_=posi), reads=["fs0"], writes=["fs1"])
    add("dve", lambda e: e.tensor_scalar(out=ang, in0=angk, scalar1=vecs[:, V_INVF:V_INVF + 1], scalar2=None,
                                         op0=ALU.mult), reads=["fs1", "vecs"], writes=["ang"])
    add("dve", lambda e: e.scalar_tensor_tensor(out=ang, in0=angk, scalar=vecs[:, V_INVF + 1:V_INVF + 2], in1=ang,
                                                op0=ALU.mult, op1=ALU.add), reads=["fs1", "ang", "vecs"], writes=["ang"])
    add("dve", lambda e: e.tensor_scalar(out=angk, in0=ang, scalar1=float(1.0 / (2 * np.pi)), scalar2=None,
                                         op0=ALU.mult), reads=["ang"], writes=["fs1"])
    add("dve", lambda e: e.tensor_copy(out=posi, in_=angk), reads=["fs1"], writes=["fs0"])
    add("dve", lambda e: e.tensor_copy(out=angk, in_=posi), reads=["fs0"], writes=["fs1"])
    C1 = 6.28125
    C2 = float(2 * np.pi - 6.28125)
    add("dve", lambda e: e.scalar_tensor_tensor(out=ang, in0=angk, scalar=-C1, in1=ang, op0=ALU.mult, op1=ALU.add),
        reads=["fs1", "ang"], writes=["ang"])
    add("dve", lambda e: e.scalar_tensor_tensor(out=ang, in0=angk, scalar=-C2, in1=ang, op0=ALU.mult, op1=ALU.add),
        reads=["fs1", "ang"], writes=["ang"])
    chk("P0a", ang)
    add("dve", lambda e: e.tensor_scalar(out=angk, in0=ang, scalar1=-PI, scalar2=PI, op0=ALU.max, op1=ALU.min),
        reads=["ang"], writes=["fs1"])
    chk("P0k", angk)
    add("act", lambda e: e.activation(out=stab, in_=angk, func=AF.Sin), reads=["fs1"], writes=["stab"])
    r2 = fv(A_FS, 2048)
    add("dve", lambda e: e.tensor_scalar(out=ang, in0=ang, scalar1=PI / 2, scalar2=None, op0=ALU.add),
        reads=["ang"], writes=["ang"])
    add("dve", lambda e: e.tensor_scalar(out=r2, in0=ang, scalar1=PI, scalar2=-2 * PI, op0=ALU.is_gt, op1=ALU.mult),
        reads=["ang"], writes=["fs0"])
    add("dve", lambda e: e.tensor_tensor(out=ang, in0=ang, in1=r2, op=ALU.add), reads=["ang", "fs0"], writes=["ang"])
    add("dve", lambda e: e.tensor_scalar(out=ang, in0=ang, scalar1=-PI, scalar2=PI, op0=ALU.max, op1=ALU.min),
        reads=["ang"], writes=["ang"])
    add("act", lambda e: e.activation(out=ctab, in_=ang, func=AF.Sin), reads=["ang"], writes=["ctab"])
    chk("P0s", stab)
    chk("P0c", ctab)
    rms_pass2_tb(xall, [("xall", c) for c in range(8)], hT, V_N1W, "hT", rstd1)
    add("pool", lambda e: e.memset(den, 0.0), writes=["den"] + [("rstd", tb) for tb in range(NT)])
    XALL_HI = [("xall", c) for c in range(4, 8)]
    chk("P1", hT)

    HT_ALL = [("hT", c) for c in range(8)]


    qw_ap = vecs[:, V_QW:V_QW + 1]
    kw_ap = vecs[:, V_KW:V_KW + 1]
    sbk = sb.rearrange("p (a t) -> p a t", a=8)
    fsk = fs.rearrange("p (a t) -> p a t", a=8)
    pT2 = sb[:, 3072:4096].rearrange("p (a t) -> p a t", a=2)
    QA, QBk, QC = (0, 1, 2, 7), (3, 4), (5, 6)

    def qk_units(g):
        units = []
        for j in range(8):
            for tb in range(NT):
                units.append((j, tb))
        n = len(units)

        def bufs(i):
            return dict(A=QA[i % 4], B=QBk[i % 2], C=QC[i % 2], sq=i % 3, qn=3 + i % 3, rs=i % 3, t1=3 + i % 2, t2=5 + i % 2)

        def e_proj(i):
            j, tb = units[i]
            b = bufs(i)
            A = psb[b["A"]]
            for kc in range(8):
                add("pe", lambda e, j=j, kc=kc, tb=tb, A=A: e.matmul(A, lhsT=wsl[:, j, kc, :], rhs=hT[:, kc, tb * TB:(tb + 1) * TB],
                                                                     start=(kc == 0), stop=(kc == 7)),
                    reads=[("w", j), ("hT", kc, tb)], writes=[("ps", b["A"])])

        def e_sq(i):
            b = bufs(i)
            A = psb[b["A"]]
            add("act", lambda e, A=A, b=b: e.activation(out=sbk[:, b["sq"], :], in_=A, func=AF.Square),
                reads=[("ps", b["A"])], writes=[("sbk", b["sq"])])

        def e_ss(i):
            b = bufs(i)
            B = psb[b["B"]]
            add("pe", lambda e, B=B, b=b: e.matmul(B, lhsT=bd_ones, rhs=sbk[:, b["sq"], :], start=True, stop=True),
                reads=[("sbk", b["sq"]), "cm"], writes=[("ps", b["B"])])

        def e_sqrt(i):
            b = bufs(i)
            B = psb[b["B"]]
            add("act", lambda e, B=B, b=b: e.activation(out=fsk[:, b["rs"], :], in_=B, func=AF.Ln, bias=cf[:, 0:1], scale=1.0 / 64),
                reads=[("ps", b["B"]), "cf"], writes=[("fsk", b["rs"])])

        def e_recip(i):
            b = bufs(i)
            add("act", lambda e, b=b: e.activation(out=fsk[:, b["rs"], :], in_=fsk[:, b["rs"], :], func=AF.Exp, scale=-0.5),
                reads=[("fsk", b["rs"])], writes=[("fsk", b["rs"])])

        def e_qn(i):
            j, tb = units[i]
            b = bufs(i)
            A = psb[b["A"]]
            wap = qw_ap if j < 4 else kw_ap
            add("dve", lambda e, A=A, b=b, wap=wap: e.scalar_tensor_tensor(out=sbk[:, b["qn"], :], in0=A, scalar=wap, in1=fsk[:, b["rs"], :],
                                                                          op0=ALU.mult, op1=ALU.mult),
                reads=[("ps", b["A"]), ("fsk", b["rs"]), "vecs"], writes=[("sbk", b["qn"])])

        def e_rq(i):
            b = bufs(i)
            Cb = psb[b["C"]]
            add("pe", lambda e, Cb=Cb, b=b: e.matmul(Cb, lhsT=rotR, rhs=sbk[:, b["qn"], :], start=True, stop=True),
                reads=[("sbk", b["qn"]), "cm"], writes=[("ps", b["C"])])

        def e_t1(i):
            j, tb = units[i]
            b = bufs(i)
            add("pool", lambda e, b=b, tb=tb: e.tensor_tensor(out=fsk[:, b["t1"], :], in0=sbk[:, b["qn"], :], in1=ctab[:, tb * TB:(tb + 1) * TB],
                                                              op=ALU.mult),
                reads=[("sbk", b["qn"]), "ctab"], writes=[("fsk", b["t1"])])

        def e_t2(i):
            j, tb = units[i]
            b = bufs(i)
            Cb = psb[b["C"]]
            add("dve", lambda e, Cb=Cb, b=b, tb=tb: e.tensor_tensor(out=fsk[:, b["t2"], :], in0=Cb, in1=stab[:, tb * TB:(tb + 1) * TB],
                                                                    op=ALU.mult),
                reads=[("ps", b["C"]), "stab"], writes=[("fsk", b["t2"])])

        def e_add(i):
            j, tb = units[i]
            b = bufs(i)
            dd = (1, 4, 16)[g]
            T = (qT if j < 4 else kT)
            if dd == 1:
                dst = T[:, j % 4, tb * TB:(tb + 1) * TB]
                s1v, s2v = fsk[:, b["t1"], :], fsk[:, b["t2"], :]
            else:
                w = TB // dd
                dst = T[:, j % 4, :].rearrange("p (r m) -> p r m", r=dd)[:, :, tb * w:(tb + 1) * w]
                s1v = fsk[:, b["t1"], :].rearrange("p (m r) -> p r m", r=dd)
                s2v = fsk[:, b["t2"], :].rearrange("p (m r) -> p r m", r=dd)
            dkey = ("qT" if j < 4 else "kT", j % 4)
            add("dve" if i % 2 == 0 else "pool", lambda e, dst=dst, s1v=s1v, s2v=s2v: e.tensor_tensor(out=dst, in0=s1v, in1=s2v, op=ALU.add),
                reads=[("fsk", b["t1"]), ("fsk", b["t2"])], writes=[dkey])

        def ok(k):
            return 0 <= k < n

        for i in range(n + 6):
            if ok(i - 2):
                e_ss(i - 2)
            if ok(i - 4):
                e_rq(i - 4)
            if ok(i):
                e_proj(i)
            if ok(i - 2):
                e_sqrt(i - 2)
            if ok(i - 1):
                e_sq(i - 1)
            if ok(i - 2):
                e_recip(i - 2)
            if ok(i - 4):
                e_t2(i - 4)
                e_t1(i - 4)
            if ok(i - 3):
                e_qn(i - 3)
            if ok(i - 5):
                e_add(i - 5)

    def key_jobs(g):
        d = (1, 4, 16)[g]
        L = S // d
        jobs = []
        if g < 2:
            nqt = L // 128
            for r in range(d):
                for b in range(nqt + 1):
                    if b == 0:
                        jobs.append(dict(k0=r, kstep=d, kp=r * L, mk=64, mask=M_FIRST, segs=[(r, 0, True, False)]))
                    elif b == nqt:
                        jobs.append(dict(k0=(L - 64) * d + r, kstep=d, kp=r * L + L - 64, mk=64, mask=M_LAST,
                                         segs=[(r, nqt - 1, False, True)]))
                    else:
                        jobs.append(dict(k0=(128 * b - 64) * d + r, kstep=d, kp=r * L + 128 * b - 64, mk=128, mask=M_MID,
                                         segs=[(r, b - 1, False, True), (r, b, True, False)]))
        else:
            for r in range(d):
                jobs.append(dict(k0=r, kstep=d, kp=r * L, mk=128, mask=M_G2, segs=[(r, 0, True, True)]))
        return jobs, d

    qt_par = {}
    qt_counter = [0]
    V_BANK, ST_PAIRS, NUM_BANKS, DEN_BANK = 4, (5, 0), (7, 3), 2
    pj_count = [0]
    vt_count = [0]

    def attention(g):
        jobs, d = key_jobs(g)
        import os
        if os.environ.get("ATT_JOBS"):
            jobs = jobs[:int(os.environ["ATT_JOBS"])]
        items = []
        for ji in range(len(jobs)):
            for pr in range(4):
                items.append((ji, pr))
        jinfo = {}

        def vproj(ji):
            job = jobs[ji]
            mk = job["mk"]
            ktok = slice(job["k0"], job["k0"] + (mk - 1) * job["kstep"] + 1, job["kstep"])
            vs = vt_count[0] % 2
            vt_count[0] += 1
            for kc in range(8):
                add("pe", lambda e, kc=kc, ktok=ktok, mk=mk: e.matmul(psb[V_BANK][0:mk, :], lhsT=hT[:, kc, ktok], rhs=wv_view[:, kc, :],
                                                                      start=(kc == 0), stop=(kc == 7)),
                    reads=[("hT", kc, 0), ("hT", kc, 1), ("hT", kc, 2), ("hT", kc, 3), ("w", 8), ("w", 9), ("w", 10), ("w", 11)], writes=[("ps", V_BANK)])
            add("act", lambda e, vs=vs, mk=mk: e.activation(out=vtile[0:mk, vs, :], in_=psb[V_BANK][0:mk, :], func=AF.Copy),
                reads=[("ps", V_BANK)], writes=[("vt", vs)])
            seginfo = []
            for (r, qt, is_start, is_stop) in job["segs"]:
                key = (g, r, qt)
                if is_start:
                    qt_par[key] = qt_counter[0] % 2
                    qt_counter[0] += 1
                par = qt_par[key]
                q0 = (128 * qt) * d + r
                qtok = slice(q0, q0 + 127 * d + 1, d)
                seginfo.append((par, qtok, is_start, is_stop))
            L_ = S // d
            r0_, qt0_ = job["segs"][0][0], job["segs"][0][1]
            qp0 = r0_ * L_ + 128 * qt0_
            kperm = slice(job["kp"], job["kp"] + mk)
            qperm = slice(qp0, qp0 + 128 * len(job["segs"]))
            jinfo[ji] = (mk, ktok, vs, seginfo, kperm, qperm)

        def stage_a(ji, pr, slot):
            mk, ktok, vs, seginfo, kperm, qperm = jinfo[ji]
            nq = 128 * len(seginfo)
            b0 = ST_PAIRS[slot]
            mcol = jobs[ji]["mask"]
            for hh in range(2):
                ST = psb[b0 + hh]
                if mk == 128:
                    add("pe", lambda e, ST=ST, nq=nq, mcol=mcol: e.matmul(ST[:, 0:nq], lhsT=ident, rhs=cm[:, mcol:mcol + nq],
                                                                          start=True, stop=False),
                        reads=["cm"], writes=[("ps", b0 + hh)])
                else:
                    r0 = 64 * hh
                    add("pe", lambda e, ST=ST, nq=nq, mcol=mcol, r0=r0: e.matmul(ST[0:64, 0:nq], lhsT=ident[r0:r0 + 64, r0:r0 + 64],
                                                                                 rhs=cm[r0:r0 + 64, mcol:mcol + nq], start=True, stop=False),
                        reads=["cm"], writes=[("ps", b0 + hh)])
            for hh in range(2):
                ST = psb[b0 + hh]
                add("pe", lambda e, ST=ST, hh=hh, pr=pr, kperm=kperm, qperm=qperm, nq=nq, mk=mk: e.matmul(
                    ST[0:mk, 0:nq], lhsT=kT[64 * hh:64 * hh + 64, pr, kperm], rhs=qT[64 * hh:64 * hh + 64, pr, qperm],
                    start=False, stop=True),
                    reads=[("kT", pr), ("qT", pr)], writes=[("ps", b0 + hh)])
            stp = ps_all[:, b0 * 512:(b0 + 2) * 512].rearrange("p (h n) -> p h n", h=2)
            pv = pT2[:, slot, :].rearrange("p (h n) -> p h n", h=2)
            add("act", lambda e, stp=stp, pv=pv, mk=mk, nq=nq: e.activation(out=pv[0:mk, :, 0:nq], in_=stp[0:mk, :, 0:nq], func=AF.Exp, scale=0.125),
                reads=[("ps", b0), ("ps", b0 + 1)], writes=[("sbk", 6 + slot)])

        def stage_b(ji, pr, slot):
            mk, ktok, vs, seginfo, kperm, qperm = jinfo[ji]
            for si, (par, qtok, is_start, is_stop) in enumerate(seginfo):
                for hh in range(2):
                    h = 2 * pr + hh
                    col = hh * 256 + si * 128
                    nb = NUM_BANKS[par]
                    add("pe", lambda e, nb=nb, hh=hh, pr=pr, h=h, vs=vs, mk=mk, slot=slot, col=col, is_start=is_start, is_stop=is_stop:
                        e.matmul(psb[nb][64 * hh:64 * hh + 64, pr * 128:(pr + 1) * 128], lhsT=vtile[0:mk, vs, h * 64:(h + 1) * 64],
                                 rhs=pT2[0:mk, slot, col:col + 128], start=(is_start and pr == 0), stop=(is_stop and pr == 3),
                                 tile_position=(0, 64 * hh)),
                        reads=[("vt", vs), ("sbk", 6 + slot)], writes=[("ps", nb)])
            for hh in range(2):
                h = 2 * pr + hh
                for si, (par, qtok, is_start, is_stop) in enumerate(seginfo):
                    col = hh * 256 + si * 128
                    add("pe", lambda e, par=par, h=h, mk=mk, slot=slot, col=col, is_start=is_start, is_stop=is_stop:
                        e.matmul(psb[DEN_BANK][32 * par:32 * par + 8, 0:128], lhsT=sel[0:mk, h, :],
                                 rhs=pT2[0:mk, slot, col:col + 128], start=(is_start and h == 0), stop=(is_stop and h == 7),
                                 tile_position=(0, 32 * par)),
                        reads=["cm", ("sbk", 6 + slot)], writes=[("psden", par)])
            if pr == 3:
                for (par, qtok, is_start, is_stop) in seginfo:
                    if is_stop:
                        nb = NUM_BANKS[par]
                        add("dve", lambda e, nb=nb, qtok=qtok: e.tensor_tensor(out=acc[:, :, qtok],
                                                                               in0=psb[nb].rearrange("p (a t) -> p a t", a=4),
                                                                               in1=acc[:, :, qtok], op=ALU.add),
                            reads=[("ps", nb), "acc"], writes=["acc"])
                        add("dve", lambda e, par=par, qtok=qtok: e.tensor_tensor(out=den[32 * par:32 * par + 8, qtok],
                                                                                 in0=psb[DEN_BANK][32 * par:32 * par + 8, 0:128],
                                                                                 in1=den[32 * par:32 * par + 8, qtok], op=ALU.add),
                            reads=[("psden", par), "den"], writes=["den"])

        n = len(items)
        for i in range(n + 1):
            if i < n:
                ji, pr = items[i]
                if pr == 0:
                    vproj(ji)
                stage_a(ji, pr, i % 2)
            if i >= 1:
                ji, pr = items[i - 1]
                stage_b(ji, pr, (i - 1) % 2)

    for g in range(3):
        if g > 0:
            load_qkv(g)
        qk_units(g)
        if g == 0:
            add("pool", lambda e: e.memset(acc, 0.0), writes=["acc"] + [("xall", c) for c in range(4)])
            chk("Q0", qT)
            chk("K0", kT)
        attention(g)
        if g == 0:
            chk("A0", acc)
            chk("D0", den)
    chk("A2", acc)
    chk("D2", den)

    for cc in range(4):
        load_w(cc, w_in_c[36 + cc])
        load_w(4 + cc, w_in_c[40 + cc])
    UW_ = 2080
    R2_OLD = [("qT", i_) for i_ in range(4)] + [("kT", i_) for i_ in range(4)] + [("vt", 0), ("vt", 1)]
    add("pool", lambda e: e.memset(uT[:, :, 0:15], 0.0), writes=["uT"] + R2_OLD)
    add("pool", lambda e: e.memset(uT[:, :, 15 + S:UW_], 0.0), writes=["uT"])
    def build_diag(cc, dg):
        idb = ident.rearrange("p (o m) -> p o m", o=1).broadcast_to([128, 31, 128])
        cwb = vecs[:, V_CW + cc * 31:V_CW + cc * 31 + 31].rearrange("p (j o) -> p j o", o=1).broadcast_to([128, 31, 128])
        add("dve", lambda e: e.tensor_tensor(out=dg[:, cc % 2, :, :], in0=idb, in1=cwb, op=ALU.mult),
            reads=["cm", "vecs"], writes=[("diag", cc)] + (["acc"] if cc >= 2 else R2_OLD))


    selb = vecs[0:40, V_SELB:V_SELB + 512].rearrange("p (a m) -> p a m", a=4)
    rdb = sp_f.rearrange("p (a t) -> p a t", a=4)
    p4 = [(pr, tb) for pr in range(4) for tb in range(NT)]

    def p4_mm(k):
        pr, tb = p4[k]
        bank = 4 + k % 4
        add("pe", lambda e: e.matmul(psb[bank], lhsT=selb[:, pr, :], rhs=den[0:40, tb * TB:(tb + 1) * TB], start=True, stop=True),
            reads=["den", "vecs"], writes=[("ps", bank)])

    def p4_ln(k):
        bank = 4 + k % 4
        add("act", lambda e: e.activation(out=rdb[:, k % 4, :], in_=psb[bank], func=AF.Ln),
            reads=[("ps", bank)], writes=[("rdb", k % 4)])

    def p4_exp(k):
        add("act", lambda e: e.activation(out=rdb[:, k % 4, :], in_=rdb[:, k % 4, :], func=AF.Exp, scale=-1.0),
            reads=[("rdb", k % 4)], writes=[("rdb", k % 4)])

    def p4_mul(k):
        pr, tb = p4[k]
        add("dve", lambda e: e.tensor_tensor(out=attnT[:, pr, tb * TB:(tb + 1) * TB], in0=rdb[:, k % 4, :],
                                             in1=acc[:, pr, tb * TB:(tb + 1) * TB], op=ALU.mult),
            reads=[("rdb", k % 4), "acc"], writes=[("attnT", pr)])

    def p4_step(k):
        if k < len(p4):
            p4_mm(k)
        if 0 <= k - 1 < len(p4):
            p4_exp(k - 1)
        if k < len(p4):
            p4_ln(k)
        if 0 <= k - 2 < len(p4):
            p4_mul(k - 2)

    u = 0
    rs2 = fs[:, 0:1024].rearrange("p (a t) -> p a t", a=2)
    for cc in range(4):
        for tb in range(NT):
            par = u % 2
            u += 1
            A, B = psb[par], psb[2 + par]
            for kc in range(8):
                add("pe", lambda e, cc=cc, kc=kc, tb=tb, A=A: e.matmul(A, lhsT=wsl[:, cc, kc, :], rhs=hT[:, kc, tb * TB:(tb + 1) * TB],
                                                                       start=(kc == 0), stop=(kc == 7)),
                    reads=[("w", cc), ("hT", kc, tb)], writes=[("ps", par)])
            for kc in range(8):
                add("pe", lambda e, cc=cc, kc=kc, tb=tb, B=B: e.matmul(B, lhsT=wsl[:, 4 + cc, kc, :], rhs=hT[:, kc, tb * TB:(tb + 1) * TB],
                                                                       start=(kc == 0), stop=(kc == 7)),
                    reads=[("w", 4 + cc), ("hT", kc, tb)], writes=[("ps", 2 + par)])
            add("act", lambda e, B=B, par=par: e.activation(out=rs2[:, par, :], in_=B, func=AF.Sigmoid),
                reads=[("ps", 2 + par)], writes=[("rs2", par)])
            add("dve", lambda e, A=A, par=par, cc=cc, tb=tb: e.tensor_tensor(out=uT[:, cc, 15 + tb * TB:15 + (tb + 1) * TB], in0=A,
                                                                             in1=rs2[:, par, :], op=ALU.mult),
                reads=[("ps", par), ("rs2", par)], writes=["uT"])
            if u == 3:
                build_diag(0, diag01)
            if u == 6:
                build_diag(1, diag01)
            if u % 4 == 0:
                for k_ in range(u - 4, u):
                    p4_step(k_)
    p4_step(16)
    p4_step(17)
    chk("P4", attnT)
    diag23 = bv(A_ACC, 2 * 31 * 128).rearrange("p (c j m) -> p c j m", c=2, j=31)
    for cc in (2, 3):
        build_diag(cc, diag23)
    def load_p5(c):
        if c >= 8:
            return
        s0 = (c % 4) * 3
        oa_v = bv(A_W + s0 * 512, 512).rearrange("p (k m) -> p k m", k=4)
        pw_v = bv(A_W + s0 * 512 + 256, 512).rearrange("p (k m) -> p k m", k=4)
        add("pool", lambda e, oa_v=oa_v, c=c: e.dma_start(out=oa_v, in_=w_oa[c].rearrange("p (k m) -> p k m", k=4)),
            writes=[("w", s0)], dma_sem="w%d" % s0)
        add("pool", lambda e, pw_v=pw_v, c=c: e.dma_start(out=pw_v, in_=w_pw[c].rearrange("p (k m) -> p k m", k=4)),
            writes=[("w", s0)], dma_sem="w%d" % s0)
        load_w(s0 + 1, w_in_c[44 + c])
        load_w(s0 + 2, w_in_c[52 + c])

    load_p5(0)
    load_p5(1)
    load_p5(2)
    ycv2 = fs.rearrange("p (b a t) -> p b a t", b=2, a=4)
    ysq = sb[:, 0:2048].rearrange("p (a t) -> p a t", a=4)
    ybf = sb[:, 2048:4096].rearrange("p (a t) -> p a t", a=4)
    mean2 = sp_f[:, 0:1024].rearrange("p (a t) -> p a t", a=2)
    rstd2 = sp_f[:, 1024:2048].rearrange("p (a t) -> p a t", a=2)

    def conv_cc(tb, cc):
        yb = tb % 2
        dg = diag01 if cc < 2 else diag23
        bank = cc % 2
        sbank = 2 + 2 * yb
        for j in range(31):
            add("pe", lambda e, dg=dg, cc=cc, j=j, tb=tb, bank=bank: e.matmul(psb[bank], lhsT=dg[:, cc % 2, j, :],
                                                                              rhs=uT[:, cc, tb * TB + j:tb * TB + j + TB],
                                                                              start=(j == 0), stop=(j == 30)),
                reads=[("diag", cc), "uT"], writes=[("ps", bank)])
        add("act", lambda e, cc=cc, bank=bank, yb=yb: e.activation(out=ycv2[:, yb, cc, :], in_=psb[bank], func=AF.Identity,
                                                                   bias=vecs[:, V_CB + cc:V_CB + cc + 1], scale=1.0),
            reads=[("ps", bank), "vecs"], writes=[("ycv", yb, cc)])
        add("act", lambda e, cc=cc, yb=yb: e.activation(out=ysq[:, cc, :], in_=ycv2[:, yb, cc, :], func=AF.Square),
            reads=[("ycv", yb, cc)], writes=[("ysq", cc)])
        add("pool", lambda e, cc=cc, yb=yb: e.tensor_copy(out=ybf[:, cc, :], in_=ycv2[:, yb, cc, :]),
            reads=[("ycv", yb, cc)], writes=[("ybf", cc)])

    def conv_stats(tb, cc):
        yb = tb % 2
        sbank = 2 + 2 * yb
        add("pe", lambda e, cc=cc, sbank=sbank: e.matmul(psb[sbank], lhsT=ones_m, rhs=ybf[:, cc, :], start=(cc == 0), stop=(cc == 3)),
            reads=[("ybf", cc), "cm"], writes=[("ps", sbank)])
        add("pe", lambda e, cc=cc, sbank=sbank: e.matmul(psb[sbank + 1], lhsT=ones_m, rhs=ysq[:, cc, :], start=(cc == 0), stop=(cc == 3)),
            reads=[("ysq", cc), "cm"], writes=[("ps", sbank + 1)])

    def ln_stats(tb):
        yb = tb % 2
        sbank = 2 + 2 * yb
        mean_t, var_t = mean2[:, yb, :], rstd2[:, yb, :]
        add("dve", lambda e: e.tensor_scalar(out=mean_t, in0=psb[sbank], scalar1=1.0 / 512, scalar2=None, op0=ALU.mult),
            reads=[("ps", sbank)], writes=[("mean", yb)])
        add("dve", lambda e: e.tensor_tensor(out=var_t, in0=mean_t, in1=mean_t, op=ALU.mult), reads=[("mean", yb)], writes=[("var", yb)])
        add("dve", lambda e: e.scalar_tensor_tensor(out=var_t, in0=psb[sbank + 1], scalar=1.0 / 512, in1=var_t, op0=ALU.mult, op1=ALU.subtract),
            reads=[("ps", sbank + 1), ("var", yb)], writes=[("var", yb)])
        add("act", lambda e: e.activation(out=var_t, in_=var_t, func=AF.Ln, bias=cf[:, 0:1], scale=1.0),
            reads=[("var", yb), "cf"], writes=[("var", yb)])
        add("act", lambda e: e.activation(out=var_t, in_=var_t, func=AF.Exp, scale=-0.5), reads=[("var", yb)], writes=[("var", yb)])

    def ln_tail(tb, cc):
        yb = tb % 2
        y = ycv2[:, yb, cc, :]
        add("dve", lambda e: e.tensor_tensor(out=y, in0=y, in1=mean2[:, yb, :], op=ALU.subtract),
            reads=[("ycv", yb, cc), ("mean", yb)], writes=[("ycv", yb, cc)])
        add("pool", lambda e: e.tensor_tensor(out=y, in0=y, in1=rstd2[:, yb, :], op=ALU.mult),
            reads=[("ycv", yb, cc), ("var", yb)], writes=[("ycv", yb, cc)])
        add("act", lambda e: e.activation(out=ucT[:, cc, tb * TB:(tb + 1) * TB], in_=y, func=AF.Silu,
                                          bias=vecs[:, V_LB + cc:V_LB + cc + 1], scale=vecs[:, V_LW + cc:V_LW + cc + 1]),
            reads=[("ycv", yb, cc), "vecs"], writes=[("ucT", cc, tb)])

    steps = [(tb, cc) for tb in range(NT) for cc in range(4)]
    for k in range(len(steps) + 6):
        if k - 6 >= 0 and k - 6 < len(steps):
            ln_tail(*steps[k - 6])
        if k < len(steps):
            conv_cc(*steps[k])
        if 0 <= k - 1 < len(steps):
            tb1, cc1 = steps[k - 1]
            conv_stats(tb1, cc1)
            if cc1 == 3:
                ln_stats(tb1)

    chk("P3", ucT)

    ga_t = fs[:, 0:1024].rearrange("p (a t) -> p a t", a=2)
    gb_t = fs[:, 1024:2048].rearrange("p (a t) -> p a t", a=2)
    ta_t = fv(A_ACC + 4096, 1024).rearrange("p (a t) -> p a t", a=2)
    tb_t = fv(A_ACC + 5120, 1024).rearrange("p (a t) -> p a t", a=2)
    u = 0
    for c in range(8):
        s0 = (c % 4) * 3
        oa_v = bv(A_W + s0 * 512, 512).rearrange("p (k m) -> p k m", k=4)
        pw_v = bv(A_W + s0 * 512 + 256, 512).rearrange("p (k m) -> p k m", k=4)
        load_p5(c + 3)
        if c == 6:
            for c6 in range(3):
                load_w(c6, w_out_c[c6])
        if c == 7:
            load_w(3, w_out_c[3])
        for tb in range(NT):
            par = u % 2
            u += 1
            YA, YB, GA, GB = psb[4 * par], psb[4 * par + 1], psb[4 * par + 2], psb[4 * par + 3]
            tok = slice(tb * TB, (tb + 1) * TB)
            for kc in range(4):
                add("pe", lambda e, kc=kc, tok=tok, YA=YA, oa_v=oa_v: e.matmul(YA, lhsT=oa_v[:, kc, :], rhs=attnT[:, kc, tok],
                                                                               start=(kc == 0), stop=(kc == 3)),
                    reads=[("w", s0), ("attnT", kc)], writes=[("ps", 4 * par)])
            for kc in range(4):
                add("pe", lambda e, kc=kc, tok=tok, YB=YB, pw_v=pw_v: e.matmul(YB, lhsT=pw_v[:, kc, :], rhs=ucT[:, kc, tok],
                                                                               start=(kc == 0), stop=(kc == 3)),
                    reads=[("w", s0), ("ucT", kc, tb)], writes=[("ps", 4 * par + 1)])
            for kc in range(8):
                add("pe", lambda e, kc=kc, tok=tok, GA=GA, s0=s0: e.matmul(GA, lhsT=wsl[:, s0 + 1, kc, :], rhs=hT[:, kc, tok],
                                                                           start=(kc == 0), stop=(kc == 7)),
                    reads=[("w", s0 + 1), ("hT", kc, tb)], writes=[("ps", 4 * par + 2)])
            for kc in range(8):
                add("pe", lambda e, kc=kc, tok=tok, GB=GB, s0=s0: e.matmul(GB, lhsT=wsl[:, s0 + 2, kc, :], rhs=hT[:, kc, tok],
                                                                           start=(kc == 0), stop=(kc == 7)),
                    reads=[("w", s0 + 2), ("hT", kc, tb)], writes=[("ps", 4 * par + 3)])
            add("act", lambda e, GA=GA, par=par, c=c: e.activation(out=ga_t[:, par, :], in_=GA, func=AF.Sigmoid,
                                                                   bias=vecs[:, V_BGA + c:V_BGA + c + 1], scale=1.0),
                reads=[("ps", 4 * par + 2), "vecs"], writes=[("ga", par), ("ycv", 0, 0), ("ycv", 0, 1)])
            add("act", lambda e, GB=GB, par=par, c=c: e.activation(out=gb_t[:, par, :], in_=GB, func=AF.Sigmoid,
                                                                   bias=vecs[:, V_BGB + c:V_BGB + c + 1], scale=1.0),
                reads=[("ps", 4 * par + 3), "vecs"], writes=[("gb", par), ("ycv", 0, 2), ("ycv", 0, 3)])
            add("dve", lambda e, YA=YA, par=par: e.tensor_tensor(out=ta_t[:, par, :], in0=YA, in1=ga_t[:, par, :], op=ALU.mult),
                reads=[("ps", 4 * par), ("ga", par)], writes=[("ta", par)])
            add("dve", lambda e, YB=YB, par=par: e.tensor_tensor(out=tb_t[:, par, :], in0=YB, in1=gb_t[:, par, :], op=ALU.mult),
                reads=[("ps", 4 * par + 1), ("gb", par)], writes=[("tb", par)])
            add("pool", lambda e, par=par, c=c, tok=tok: e.tensor_tensor(out=zT[:, c, tok], in0=ta_t[:, par, :], in1=tb_t[:, par, :], op=ALU.add),
                reads=[("ta", par), ("tb", par)], writes=[("zT", c), "uT", ("diag", 0), ("diag", 1)])

    chk("P5", zT)
    barrier()
    xst6 = fv(A_R3, 4096).rearrange("p (a t) -> p a t", a=2)
    rstd7 = fv(A_R3 + 4096, 2048)
    u = 0
    for c in range(9):
        if c < 8:
            slot = c % 4
            if c >= 4:
                load_w(slot, w_out_c[c])
            xs = c % 2
            add("sp", lambda e, c=c, xs=xs: e.dma_start(out=xst6[:, xs, :], in_=xT[c * 128:(c + 1) * 128, :]),
                writes=[("xst6", xs)], dma_sem="xst%d" % xs)
            for tb in range(NT):
                bank = u % 4
                u += 1
                tok = slice(tb * TB, (tb + 1) * TB)
                for kc in range(8):
                    add("pe", lambda e, kc=kc, tok=tok, bank=bank, slot=slot: e.matmul(psb[bank], lhsT=wsl[:, slot, kc, :], rhs=zT[:, kc, tok],
                                                                                       start=(kc == 0), stop=(kc == 7)),
                        reads=[("w", slot), ("zT", kc)], writes=[("ps", bank)])
                add("dve", lambda e, c=c, tok=tok, bank=bank, xs=xs: e.tensor_tensor(out=x1T[:, c, tok], in0=psb[bank], in1=xst6[:, xs, tok], op=ALU.add),
                    reads=[("ps", bank), ("xst6", xs)], writes=[("x1T", c)])
        if c >= 1:
            rms_pass1(c - 1, x1T[:, c - 1, :], ("x1T", c - 1), 4)
    chk("P6", x1T)
    rms_rstd(rstd7, 4)
    rms_pass2_tb([x1T[:, c, :] for c in range(8)], [("x1T", c) for c in range(8)], hT, V_N2W, "h2T", rstd7)

    chk("P7", hT)
    sg_t = fv(A_R3 + 6144, 1024).rearrange("p (a t) -> p a t", a=2)
    groups = [list(range(0, 8)), list(range(8, 16)), list(range(16, 22))]
    u = 0
    fcount = 0
    out_ops = []
    all_f = [f for grp in groups for f in grp]

    def load_fi(k):
        if k < len(all_f):
            sgk = 8 + 2 * (k % 2)
            load_w(sgk, w_fi[all_f[k]])
            load_w(sgk + 1, w_fi[22 + all_f[k]])

    load_fi(0)
    for gi, grp in enumerate(groups):
        for fi, f in enumerate(grp):
            sg_ = 8 + 2 * (fcount % 2)
            fcount += 1
            load_fi(fcount)
            if fi == 1:
                for fj, f2 in enumerate(grp):
                    load_w(fj, w_fo[f2])
            for tb in range(NT):
                par = u % 2
                u += 1
                tok = slice(tb * TB, (tb + 1) * TB)
                GT, UP = psb[2 * par], psb[2 * par + 1]
                for kc in range(8):
                    add("pe", lambda e, kc=kc, tok=tok, GT=GT, sg_=sg_: e.matmul(GT, lhsT=wsl[:, sg_, kc, :], rhs=hT[:, kc, tok],
                                                                                 start=(kc == 0), stop=(kc == 7)),
                        reads=[("w", sg_), ("h2T", kc, tb)], writes=[("ps", 2 * par)])
                for kc in range(8):
                    add("pe", lambda e, kc=kc, tok=tok, UP=UP, sg_=sg_: e.matmul(UP, lhsT=wsl[:, sg_ + 1, kc, :], rhs=hT[:, kc, tok],
                                                                                 start=(kc == 0), stop=(kc == 7)),
                        reads=[("w", sg_ + 1), ("h2T", kc, tb)], writes=[("ps", 2 * par + 1)])
                add("act", lambda e, GT=GT, par=par: e.activation(out=sg_t[:, par, :], in_=GT, func=AF.Silu),
                    reads=[("ps", 2 * par)], writes=[("sg", par)])
                add("dve", lambda e, UP=UP, par=par, fi=fi, tok=tok: e.tensor_tensor(out=aT[:, fi, tok], in0=UP, in1=sg_t[:, par, :], op=ALU.mult),
                    reads=[("ps", 2 * par + 1), ("sg", par)], writes=[("aT", fi)])
        last = gi == len(groups) - 1
        for c in range(8):
            for tb in range(NT):
                bank = 4 + (u % 4)
                u += 1
                tok = slice(tb * TB, (tb + 1) * TB)
                for fi in range(len(grp)):
                    add("pe", lambda e, fi=fi, c=c, tok=tok, bank=bank, n=len(grp): e.matmul(
                        psb[bank], lhsT=bv(A_W + fi * 512, 1024)[:, c * 128:(c + 1) * 128], rhs=aT[:, fi, tok],
                        start=(fi == 0), stop=(fi == n - 1)),
                        reads=[("w", fi), ("aT", fi)], writes=[("ps", bank)])
                add("dve", lambda e, c=c, tok=tok, bank=bank: e.tensor_tensor(out=x1T[:, c, tok], in0=psb[bank], in1=x1T[:, c, tok], op=ALU.add),
                    reads=[("ps", bank), ("x1T", c)], writes=[("x1T", c)] + ([("x1o", c, tb)] if last else []))
                if last:
                    out_ops.append(add("sp", lambda e, c=c, tok=tok: e.dma_start(out=outT[c * 128:(c + 1) * 128, tok], in_=x1T[:, c, tok]),
                                       reads=[("x1o", c, tb)], writes=[("out", c, tb)], dma_sem="out"))


def _emit(L):
    nc, sc, es = L["nc"], L["sc"], L["es"]
    sc.finalize()
    sem_names = ["pe", "act", "dve", "pool", "sp"] + sorted({op.sem_key for op in sc.ops if op.is_dma})
    sems = {n: es.enter_context(nc.semaphore("s_" + n)) for n in sem_names}
    out_total = sc.totals.get("out", 0)

    def emitter(engname):
        def f(e):
            for op in sc.ops:
                if op.eng != engname:
                    continue
                for (sn, v) in op.waits:
                    e.wait_ge(sems[sn], v)
                if op.fn is None:
                    continue
                ins = op.fn(e)
                if op.is_dma:
                    ins.then_inc(sems[op.sem_key], 16)
                elif op.signal:
                    ins.then_inc(sems[op.eng], 1)
            if engname == "sp":
                e.wait_ge(sems["out"], out_total)
        return f

    with nc.Block() as block:
        block.sync(emitter("sp"))
        block.tensor(emitter("pe"))
        block.scalar(emitter("act"))
        block.vector(emitter("dve"))
        block.gpsimd(emitter("pool"))


def _consts():
    cmh = np.zeros((128, NCM), np.float32)
    cmh[:, M_ID:M_ID + 128] = np.eye(128, dtype=np.float32)
    bd = np.zeros((128, 128), np.float32)
    bd[:64, :64] = 1.0
    bd[64:, 64:] = 1.0
    cmh[:, M_BD:M_BD + 128] = bd
    cmh[:, M_ONES:M_ONES + 128] = 1.0
    R = np.zeros((128, 128), np.float32)
    for base in (0, 64):
        for i in range(8):
            R[base + i + 8, base + i] = -1.0
            R[base + i, base + i + 8] = 1.0
    cmh[:, M_ROT:M_ROT + 128] = R
    i = np.arange(128)[:, None]
    jq = np.arange(256)[None, :]
    mid = np.where((jq >= i) & (jq <= i + 128), 0.0, NEG).astype(np.float32)
    cmh[:, M_MID:M_MID + 256] = mid
    cmh[:, M_MID + 256:M_MID + 512] = mid
    j1 = np.arange(128)[None, :]
    first = np.where(np.abs(j1 - i) <= 64, 0.0, NEG).astype(np.float32)
    first[64:] = first[:64]
    cmh[:, M_FIRST:M_FIRST + 128] = first
    cmh[:, M_FIRST + 128:M_FIRST + 256] = first
    last = np.where(j1 >= i, 0.0, NEG).astype(np.float32)
    last[64:] = last[:64]
    cmh[:, M_LAST:M_LAST + 128] = last
    cmh[:, M_LAST + 128:M_LAST + 256] = last
    g2 = np.where(np.abs(j1 - i) <= 64, 0.0, NEG).astype(np.float32)
    cmh[:, M_G2:M_G2 + 128] = g2
    cmh[:, M_G2 + 128:M_G2 + 256] = g2
    selm = np.zeros((128, 8, 8), np.float32)
    for h in range(8):
        selm[:, h, h] = 1.0
    cmh[:, M_SEL:M_SEL + 64] = selm.reshape(128, 64)
    return cmh


_NC_CACHE = {}


def kernel(x, positions, norm1_w, w_in, b_gate, q_norm_w, k_norm_w, w_o_attn, conv_w, conv_b, conv_ln_w,
           conv_ln_b, w_pw_conv, w_out, norm2_w, w_ffn_in, w_ffn_out, _prep_only=False):
    f32 = np.float32
    x = np.asarray(x, f32)
    positions = np.asarray(positions, np.int32)
    B = x.shape[0]
    l = 0
    vec = np.zeros((128, NV), f32)
    vec[:, V_N1W:V_N1W + 8] = np.asarray(norm1_w, f32)[l].reshape(8, 128).T
    vec[:, V_N2W:V_N2W + 8] = np.asarray(norm2_w, f32)[l].reshape(8, 128).T
    vec[:, V_QW] = np.tile(np.asarray(q_norm_w, f32)[l], 2)
    vec[:, V_KW] = np.tile(np.asarray(k_norm_w, f32)[l], 2)
    bg = np.asarray(b_gate, f32)[l]
    vec[:, V_BGA:V_BGA + 8] = bg[0].reshape(8, 128).T
    vec[:, V_BGB:V_BGB + 8] = bg[1].reshape(8, 128).T
    vec[:, V_CB:V_CB + 4] = np.asarray(conv_b, f32)[l].reshape(4, 128).T
    vec[:, V_LW:V_LW + 4] = np.asarray(conv_ln_w, f32)[l].reshape(4, 128).T
    vec[:, V_LB:V_LB + 4] = np.asarray(conv_ln_b, f32)[l].reshape(4, 128).T
    invf = np.zeros(128, f32)
    invf_lo = np.zeros(128, f32)
    inv_freq64 = 500000.0 ** (-np.arange(0, 16, 2, dtype=np.float64) / 16.0)
    inv_hi = inv_freq64.astype(f32)
    inv_lo = (inv_freq64 - inv_hi.astype(np.float64)).astype(f32)
    for base in (0, 64):
        invf[base:base + 8] = inv_hi
        invf[base + 8:base + 16] = inv_hi
        invf_lo[base:base + 8] = inv_lo
        invf_lo[base + 8:base + 16] = inv_lo
    vec[:, V_INVF] = invf
    vec[:, V_INVF + 1] = invf_lo
    cw = np.asarray(conv_w, f32)[l]
    vec[:, V_CW:V_CW + 124] = cw.T.reshape(4, 128, 31).transpose(1, 0, 2).reshape(128, 124)
    selb = np.zeros((8, 4, 128), f32)
    for pr in range(4):
        selb[2 * pr, pr, :64] = 1.0
        selb[2 * pr + 1, pr, 64:] = 1.0
    vec[0:8, V_SELB:V_SELB + 512] = selb.reshape(8, 512)
    vec[32:40, V_SELB:V_SELB + 512] = selb.reshape(8, 512)
    cmh = _consts()

    def chunk_lhsT(w):
        K, N = w.shape
        return np.ascontiguousarray(w.reshape(K // 128, 128, N // 128, 128).transpose(2, 1, 0, 3)).reshape(N // 128, 128, (K // 128) * 128)

    w_in0 = np.asarray(w_in, f32)[l]
    w_in_c = chunk_lhsT(w_in0)
    w_v = np.ascontiguousarray(w_in0[:, 3072:4608].reshape(8, 128, 3, 512).transpose(2, 1, 0, 3)).reshape(3, 128, 4096)
    w_oa = chunk_lhsT(np.asarray(w_o_attn, f32)[l])
    w_pw = chunk_lhsT(np.asarray(w_pw_conv, f32)[l])
    w_out_c = chunk_lhsT(np.asarray(w_out, f32)[l])
    w_fi = chunk_lhsT(np.asarray(w_ffn_in, f32)[l])
    w_fo = np.ascontiguousarray(np.asarray(w_ffn_out, f32)[l].reshape(22, 128, 1024))

    in_maps = []
    for b in range(B):
        in_maps.append({
            "xT": np.ascontiguousarray(x[b].T),
            "posr": np.ascontiguousarray(np.broadcast_to(positions[b][None, :], (128, S))),
            "vecs": vec, "cmats": cmh, "w_in_c": w_in_c, "w_v": w_v, "w_oa": w_oa, "w_pw": w_pw,
            "w_out_c": w_out_c, "w_fi": w_fi, "w_fo": w_fo,
        })
    if _prep_only:
        return in_maps
    if "nc" not in _NC_CACHE:
        _NC_CACHE["nc"] = build_program()
    nc = _NC_CACHE["nc"]
    res = run_bass_kernel_spmd(nc, in_maps, core_ids=list(range(B)))
    out = np.stack([np.ascontiguousarray(np.asarray(r["outT"], f32).T) for r in res.results], axis=0)
    return out.astype(f32)
```

```python
import numpy as np
import concourse.bass as bass
import concourse.mybir as mybir
from concourse.bass_utils import run_bass_kernel_spmd

F32 = mybir.dt.float32
BF16 = mybir.dt.bfloat16
I32 = mybir.dt.int32
ALU = mybir.AluOpType
AF = mybir.ActivationFunctionType

S = 2048
D = 1024
NT = 4
TB = 512
EPS = 1e-6
NEG = -30000.0
PI = float(np.pi)

V_N1W, V_N2W, V_QW, V_KW, V_BGA, V_BGB, V_CB, V_LW, V_LB, V_INVF, V_CW, V_SELB = 0, 8, 16, 17, 18, 26, 34, 38, 42, 46, 48, 172
NV = 172 + 512
M_ID, M_BD, M_ONES, M_ROT, M_MID, M_FIRST, M_LAST, M_G2, M_SEL = 0, 128, 256, 384, 512, 1024, 1280, 1536, 1792
NCM = 1792 + 64

A_ACC = 0
A_DEN = 8192
A_FS = 10240
A_SP = 14336
A_X1 = 0
A_HT = 16384
A_R2 = 24576
A_R3 = 33280
A_W = 41472
A_SB = 47616
A_END = 49664


class Op:
    __slots__ = ("eng", "fn", "deps", "signal", "val", "is_dma", "sem_key", "waits", "idx")


class Sched:
    def __init__(self):
        self.ops = []
        self.last_w = {}
        self.readers = {}
        self.barrier_op = None

    def add(self, eng, fn, reads=(), writes=(), dma_sem=None):
        op = Op()
        op.eng = eng
        op.fn = fn
        op.signal = False
        op.val = 0
        op.is_dma = dma_sem is not None
        op.sem_key = dma_sem
        op.idx = len(self.ops)
        writes = list(writes)
        if ("ps", 2) in writes:
            writes += [("psden", 0), ("psden", 1)]
        elif ("psden", 0) in writes or ("psden", 1) in writes:
            writes += [("ps", 2)]
        deps = {}
        if self.barrier_op is not None:
            deps[self.barrier_op.idx] = self.barrier_op
        for k in reads:
            w = self.last_w.get(k)
            if w is not None:
                deps[w.idx] = w
        for k in writes:
            w = self.last_w.get(k)
            if w is not None:
                deps[w.idx] = w
            for r in self.readers.get(k, {}).values():
                if isinstance(r, list):
                    for rr in r:
                        deps[rr.idx] = rr
                else:
                    deps[r.idx] = r
        out = []
        for d in deps.values():
            if d is op:
                continue
            if (not d.is_dma) and (not op.is_dma) and d.eng == "pe" and eng == "pe":
                continue
            out.append(d)
        op.deps = out
        for k in reads:
            rd = self.readers.setdefault(k, {})
            if op.is_dma:
                rd.setdefault("dma", []).append(op)
            else:
                rd[eng] = op
        for k in writes:
            self.last_w[k] = op
            self.readers[k] = {}
        self.ops.append(op)
        return op

    def finalize(self):
        for op in self.ops:
            for d in op.deps:
                d.signal = True
        cnt = {}
        for op in self.ops:
            if op.is_dma:
                cnt[op.sem_key] = cnt.get(op.sem_key, 0) + 16
                op.val = cnt[op.sem_key]
            elif op.signal:
                cnt[op.eng] = cnt.get(op.eng, 0) + 1
                op.val = cnt[op.eng]
        self.totals = cnt
        known = {}
        for op in self.ops:
            kn = known.setdefault(op.eng, {})
            w = {}
            for d in op.deps:
                sn = d.sem_key if d.is_dma else d.eng
                if d.val > w.get(sn, 0):
                    w[sn] = d.val
            op.waits = []
            for sn, v in w.items():
                if v > kn.get(sn, 0):
                    op.waits.append((sn, v))
                    kn[sn] = v


class _Stop(Exception):
    pass


def build_program(stop=None):
    nc = bass.Bass("TRN2", target_bir_lowering=False)
    dt = nc.dram_tensor
    xT = dt("xT", [D, S], F32, kind="ExternalInput").ap()
    posr = dt("posr", [128, S], I32, kind="ExternalInput").ap()
    vecs_d = dt("vecs", [128, NV], F32, kind="ExternalInput").ap()
    cm_d = dt("cmats", [128, NCM], F32, kind="ExternalInput").ap()
    w_in_c = dt("w_in_c", [60, 128, 1024], F32, kind="ExternalInput").ap()
    w_v = dt("w_v", [3, 128, 4096], F32, kind="ExternalInput").ap()
    w_oa = dt("w_oa", [8, 128, 512], F32, kind="ExternalInput").ap()
    w_pw = dt("w_pw", [8, 128, 512], F32, kind="ExternalInput").ap()
    w_out_c = dt("w_out_c", [8, 128, 1024], F32, kind="ExternalInput").ap()
    w_fi = dt("w_fi", [44, 128, 1024], F32, kind="ExternalInput").ap()
    w_fo = dt("w_fo", [22, 128, 1024], F32, kind="ExternalInput").ap()
    outT = dt("outT", [D, S], F32, kind="ExternalOutput").ap()

    from contextlib import ExitStack
    es = ExitStack()
    arena_t = es.enter_context(nc.sbuf_tensor("arena", [128, A_END], F32))
    vecs_t = es.enter_context(nc.sbuf_tensor("vecs_sb", [128, NV], F32))
    cm_t = es.enter_context(nc.sbuf_tensor("cm_sb", [128, NCM], BF16))
    cf_t = es.enter_context(nc.sbuf_tensor("cf_sb", [128, 8], F32))
    ps_t = es.enter_context(nc.psum_tensor("ps_all", [128, 4096], F32))
    arena = arena_t[:]
    vecs = vecs_t[:]
    cm = cm_t[:]
    cf = cf_t[:]
    ps_all = ps_t[:]
    psb = [ps_all[:, i * 512:(i + 1) * 512] for i in range(8)]

    def fv(off, n):
        return arena[:, off:off + n]

    def bv(off_words, n_elems):
        return arena[:, off_words:off_words + n_elems // 2].bitcast(BF16)

    acc = fv(A_ACC, 8192).rearrange("p (a t) -> p a t", a=4)
    den = fv(A_DEN, 2048)
    fs = fv(A_FS, 4096)
    sp_f = fv(A_SP, 2048)
    x1T = fv(A_X1, 16384).rearrange("p (a t) -> p a t", a=8)
    hT = bv(A_HT, 16384).rearrange("p (a t) -> p a t", a=8)
    qT = bv(A_R2, 8192).rearrange("p (a t) -> p a t", a=4)
    kT = bv(A_R2 + 4096, 8192).rearrange("p (a t) -> p a t", a=4)
    vtile = bv(A_R2 + 8192, 1024).rearrange("p (a t) -> p a t", a=2)
    UW = 2080
    uT = bv(A_R2, 4 * UW).rearrange("p (a t) -> p a t", a=4)
    diag01 = bv(A_R2 + 4160, 2 * 31 * 128).rearrange("p (c j m) -> p c j m", c=2, j=31)
    diag23 = bv(A_R3 + 4096, 2 * 31 * 128).rearrange("p (c j m) -> p c j m", c=2, j=31)
    zT = bv(A_R2, 16384).rearrange("p (a t) -> p a t", a=8)
    aT = bv(A_R2, 16384).rearrange("p (a t) -> p a t", a=8)
    ctab = fv(A_R3, 2048)
    stab = fv(A_R3 + 2048, 2048)
    xst1 = fv(A_R3 + 4096, 4096).rearrange("p (a t) -> p a t", a=2)
    ucT = bv(A_R3, 8192).rearrange("p (a t) -> p a t", a=4)
    attnT = bv(A_R3 + 4096, 8192).rearrange("p (a t) -> p a t", a=4)
    wsl = bv(A_W, 12288).rearrange("p (s k m) -> p s k m", s=12, k=8)
    wv_view = bv(A_W + 4096, 4096).rearrange("p (k n) -> p k n", k=8)
    sb = bv(A_SB, 4096)

    ident = cm[:, M_ID:M_ID + 128]
    bd_ones = cm[:, M_BD:M_BD + 128]
    ones_m = cm[:, M_ONES:M_ONES + 128]
    rotR = cm[:, M_ROT:M_ROT + 128]
    sel = cm[:, M_SEL:M_SEL + 64].rearrange("p (h j) -> p h j", h=8)

    sc = Sched()
    add = sc.add
    outT3 = outT.rearrange("(a p) t -> p a t", p=128)

    def chk(name, ap):
        if stop != name:
            return
        if len(ap.shape) == 2:
            dst = outT[0:ap.shape[0], 0:ap.shape[1]]
        else:
            dst = outT3[0:ap.shape[0], 0:ap.shape[1], 0:ap.shape[2]]
        keys = list(dict.fromkeys(list(sc.last_w.keys()) + list(sc.readers.keys())))
        add("pool", lambda e: e.dma_start(out=dst, in_=ap), reads=keys, writes=[("out", 0)], dma_sem="out")
        raise _Stop()

    try:
        _phases(locals())
    except _Stop:
        pass
    _emit(locals())
    es.close()
    return nc


def _phases(L):
    globals_ = L
    (nc, add, sc, chk, fv, bv, arena, vecs, cm, cf, psb, acc, den, fs, sp_f, x1T, hT, qT, kT, vtile, uT, diag01, diag23, zT, aT,
     ctab, stab, xst1, ucT, attnT, wsl, wv_view, sb, ident, bd_ones, ones_m, rotR, sel, xT, posr, vecs_d, cm_d, w_in_c, w_v,
     w_oa, w_pw, w_out_c, w_fi, w_fo, outT, ps_all) = [L[k] for k in (
        "nc", "add", "sc", "chk", "fv", "bv", "arena", "vecs", "cm", "cf", "psb", "acc", "den", "fs", "sp_f", "x1T", "hT", "qT", "kT",
        "vtile", "uT", "diag01", "diag23", "zT", "aT", "ctab", "stab", "xst1", "ucT", "attnT", "wsl", "wv_view", "sb", "ident",
        "bd_ones", "ones_m", "rotR", "sel", "xT", "posr", "vecs_d", "cm_d", "w_in_c", "w_v", "w_oa", "w_pw", "w_out_c", "w_fi", "w_fo",
        "outT", "ps_all")]

    add("sp", lambda e: e.dma_start(out=vecs, in_=vecs_d), writes=["vecs"], dma_sem="const")
    add("pool", lambda e: e.dma_start(out=cm, in_=cm_d), writes=["cm"], dma_sem="constb")
    add("pool", lambda e: e.memset(cf[:, 0:1], EPS), writes=["cf"])

    def load_w(slot, src, n=1024, after=()):
        dst = bv(A_W + slot * 512, n)
        add("pool", lambda e: e.dma_start(out=dst, in_=src), reads=list(after), writes=[("w", slot + i) for i in range(n // 1024)],
            dma_sem="w%d" % slot)

    def load_qkv(g, after=()):
        for hp in range(4):
            load_w(hp, w_in_c[g * 4 + hp], after=(after if hp > 0 else ()))
        for hp in range(4):
            load_w(4 + hp, w_in_c[12 + g * 4 + hp], after=after)
        load_w(8, w_v[g], n=4096, after=after)

    def barrier():
        keys = list(sc.last_w.keys()) + list(sc.readers.keys())
        keys = list(dict.fromkeys(keys))
        sc.barrier_op = None
        sc.barrier_op = add("pool", lambda e: e.nop(), reads=[], writes=keys)

    def rms_pass1(c, xap, xkey, bank0):
        sqb = sb[:, 0:4096].rearrange("p (a t) -> p a t", a=2)
        add("act", lambda e, xap=xap, c=c: e.activation(out=sqb[:, c % 2, :], in_=xap, func=AF.Square),
            reads=[xkey], writes=[("sq", c % 2)])
        for tb in range(NT):
            add("pe", lambda e, c=c, tb=tb: e.matmul(psb[bank0 + tb], lhsT=ones_m, rhs=sqb[:, c % 2, tb * TB:(tb + 1) * TB],
                                                     start=(c == 0), stop=(c == 7)),
                reads=[("sq", c % 2), "cm"], writes=[("ps", bank0 + tb)])

    def rms_rstd(rstd, bank0):
        for tb in range(NT):
            add("act", lambda e, tb=tb: e.activation(out=rstd[:, tb * TB:(tb + 1) * TB], in_=psb[bank0 + tb], func=AF.Ln,
                                                     bias=cf[:, 0:1], scale=1.0 / D),
                reads=[("ps", bank0 + tb), "cf"], writes=[("rstd", tb)])
            add("act", lambda e, tb=tb: e.activation(out=rstd[:, tb * TB:(tb + 1) * TB], in_=rstd[:, tb * TB:(tb + 1) * TB],
                                                     func=AF.Exp, scale=-0.5),
                reads=[("rstd", tb)], writes=[("rstd", tb)])

    def rms_pass2(c, xap, xkey, dst, wcol, tag, rstd):
        add("dve", lambda e, xap=xap, c=c: e.scalar_tensor_tensor(out=dst[:, c, :], in0=xap, scalar=vecs[:, wcol + c:wcol + c + 1],
                                                                  in1=rstd, op0=ALU.mult, op1=ALU.mult),
            reads=[xkey, "vecs"] + [("rstd", tb) for tb in range(NT)], writes=[(tag, c)])

    def rms_pass2_tb(xaps, xkeys, dst, wcol, tag, rstd):
        for tb in range(NT):
            tok = slice(tb * TB, (tb + 1) * TB)
            for c in range(8):
                add("dve", lambda e, c=c, tok=tok: e.scalar_tensor_tensor(out=dst[:, c, tok], in0=xaps[c][:, tok],
                                                                          scalar=vecs[:, wcol + c:wcol + c + 1], in1=rstd[:, tok],
                                                                          op0=ALU.mult, op1=ALU.mult),
                    reads=[xkeys[c], "vecs", ("rstd", tb)], writes=[(tag, c, tb)])

    def rms_phase(load_chunk, dst, wcol, stage_key, tag, rstd, bank0=0):
        for c in range(8):
            xap, xkey = load_chunk(c, 0)
            rms_pass1(c, xap, xkey, bank0)
        rms_rstd(rstd, bank0)
        for c in range(8):
            xap, xkey = load_chunk(c, 1)
            rms_pass2(c, xap, xkey, dst, wcol, tag, rstd)

    xall = [fv(A_ACC + c * 2048, 2048) if c < 4 else fv(A_R2 + (c - 4) * 2048, 2048) for c in range(8)]
    posi = fv(A_FS, 2048).bitcast(I32)
    angk = fv(A_FS + 2048, 2048)
    ang = sp_f
    for c in range(8):
        add("sp", lambda e, c=c: e.dma_start(out=xall[c], in_=xT[c * 128:(c + 1) * 128, :]),
            writes=[("xall", c)], dma_sem="xa%d" % c)
        if c == 1:
            add("sp", lambda e: e.dma_start(out=posi, in_=posr), writes=["fs0"], dma_sem="xst0")
    load_qkv(0, after=[("xall", 7)])
    for c in range(8):
        rms_pass1(c, xall[c], ("xall", c), 0)
    rstd1 = fv(A_DEN, 2048)
    rms_rstd(rstd1, 0)
    add("dve", lambda e: e.tensor_copy(out=angk, in_=posi), reads=["fs0"], writes=["fs1"])
    add("dve", lambda e: e.tensor_scalar(out=ang, in0=angk, scalar1=vecs[:, V_INVF:V_INVF + 1], scalar2=None,
                                         op0=ALU.mult), reads=["fs1", "vecs"], writes=["ang"])
    add("dve", lambda e: e.scalar_tensor_tensor(out=ang, in0=angk, scalar=vecs[:, V_INVF + 1:V_INVF + 2], in1=ang,
                                                op0=ALU.mult, op1=ALU.add), reads=["fs1", "ang", "vecs"], writes=["ang"])
    add("dve", lambda e: e.tensor_scalar(out=angk, in0=ang, scalar1=float(1.0 / (2 * np.pi)), scalar2=None,
                                         op0=ALU.mult), reads=["ang"], writes=["fs1"])
    add("dve", lambda e: e.tensor_copy(out=posi, in_=angk), reads=["fs1"], writes=["fs0"])
    add("dve", lambda e: e.tensor_copy(out=angk, in_=posi), reads=["fs0"], writes=["fs1"])
    C1 = 6.28125
    C2 = float(2 * np.pi - 6.28125)
    add("dve", lambda e: e.scalar_tensor_tensor(out=ang, in0=angk, scalar=-C1, in1=ang, op0=ALU.mult, op1=ALU.add),
        reads=["fs1", "ang"], writes=["ang"])
    add("dve", lambda e: e.scalar_tensor_tensor(out=ang, in0=angk, scalar=-C2, in1=ang, op0=ALU.mult, op1=ALU.add),
        reads=["fs1", "ang"], writes=["ang"])
    chk("P0a", ang)
    add("dve", lambda e: e.tensor_scalar(out=angk, in0=ang, scalar1=-PI, scalar2=PI, op0=ALU.max, op1=ALU.min),
        reads=["ang"], writes=["fs1"])
    chk("P0k", angk)
    add("act", lambda e: e.activation(out=stab, in_=angk, func=AF.Sin), reads=["fs1"], writes=["stab"])
    r2 = fv(A_FS, 2048)
    add("dve", lambda e: e.tensor_scalar(out=ang, in0=ang, scalar1=PI / 2, scalar2=None, op0=ALU.add),
        reads=["ang"], writes=["ang"])
    add("dve", lambda e: e.tensor_scalar(out=r2, in0=ang, scalar1=PI, scalar2=-2 * PI, op0=ALU.is_gt, op1=ALU.mult),
        reads=["ang"], writes=["fs0"])
    add("dve", lambda e: e.tensor_tensor(out=ang, in0=ang, in1=r2, op=ALU.add), reads=["ang", "fs0"], writes=["ang"])
    add("dve", lambda e: e.tensor_scalar(out=ang, in0=ang, scalar1=-PI, scalar2=PI, op0=ALU.max, op1=ALU.min),
        reads=["ang"], writes=["ang"])
    add("act", lambda e: e.activation(out=ctab, in_=ang, func=AF.Sin), reads=["ang"], writes=["ctab"])
    chk("P0s", stab)
    chk("P0c", ctab)
    rms_pass2_tb(xall, [("xall", c) for c in range(8)], hT, V_N1W, "hT", rstd1)
    add("pool", lambda e: e.memset(den, 0.0), writes=["den"] + [("rstd", tb) for tb in range(NT)])
    XALL_HI = [("xall", c) for c in range(4, 8)]
    chk("P1", hT)

    HT_ALL = [("hT", c) for c in range(8)]


    qw_ap = vecs[:, V_QW:V_QW + 1]
    kw_ap = vecs[:, V_KW:V_KW + 1]
    sbk = sb.rearrange("p (a t) -> p a t", a=8)
    fsk = fs.rearrange("p (a t) -> p a t", a=8)
    pT2 = sb[:, 3072:4096].rearrange("p (a t) -> p a t", a=2)
    QA, QBk, QC = (0, 1, 2, 7), (3, 4), (5, 6)

    def qk_units(g):
        units = []
        for j in range(8):
            for tb in range(NT):
                units.append((j, tb))
        n = len(units)

        def bufs(i):
            return dict(A=QA[i % 4], B=QBk[i % 2], C=QC[i % 2], sq=i % 3, qn=3 + i % 3, rs=i % 3, t1=3 + i % 2, t2=5 + i % 2)

        def e_proj(i):
            j, tb = units[i]
            b = bufs(i)
            A = psb[b["A"]]
            for kc in range(8):
                add("pe", lambda e, j=j, kc=kc, tb=tb, A=A: e.matmul(A, lhsT=wsl[:, j, kc, :], rhs=hT[:, kc, tb * TB:(tb + 1) * TB],
                                                                     start=(kc == 0), stop=(kc == 7)),
                    reads=[("w", j), ("hT", kc, tb)], writes=[("ps", b["A"])])

        def e_sq(i):
            b = bufs(i)
            A = psb[b["A"]]
            add("act", lambda e, A=A, b=b: e.activation(out=sbk[:, b["sq"], :], in_=A, func=AF.Square),
                reads=[("ps", b["A"])], writes=[("sbk", b["sq"])])

        def e_ss(i):
            b = bufs(i)
            B = psb[b["B"]]
            add("pe", lambda e, B=B, b=b: e.matmul(B, lhsT=bd_ones, rhs=sbk[:, b["sq"], :], start=True, stop=True),
                reads=[("sbk", b["sq"]), "cm"], writes=[("ps", b["B"])])

        def e_sqrt(i):
            b = bufs(i)
            B = psb[b["B"]]
            add("act", lambda e, B=B, b=b: e.activation(out=fsk[:, b["rs"], :], in_=B, func=AF.Ln, bias=cf[:, 0:1], scale=1.0 / 64),
                reads=[("ps", b["B"]), "cf"], writes=[("fsk", b["rs"])])

        def e_recip(i):
            b = bufs(i)
            add("act", lambda e, b=b: e.activation(out=fsk[:, b["rs"], :], in_=fsk[:, b["rs"], :], func=AF.Exp, scale=-0.5),
                reads=[("fsk", b["rs"])], writes=[("fsk", b["rs"])])

        def e_qn(i):
            j, tb = units[i]
            b = bufs(i)
            A = psb[b["A"]]
            wap = qw_ap if j < 4 else kw_ap
            add("dve", lambda e, A=A, b=b, wap=wap: e.scalar_tensor_tensor(out=sbk[:, b["qn"], :], in0=A, scalar=wap, in1=fsk[:, b["rs"], :],
                                                                          op0=ALU.mult, op1=ALU.mult),
                reads=[("ps", b["A"]), ("fsk", b["rs"]), "vecs"], writes=[("sbk", b["qn"])])

        def e_rq(i):
            b = bufs(i)
            Cb = psb[b["C"]]
            add("pe", lambda e, Cb=Cb, b=b: e.matmul(Cb, lhsT=rotR, rhs=sbk[:, b["qn"], :], start=True, stop=True),
                reads=[("sbk", b["qn"]), "cm"], writes=[("ps", b["C"])])

        def e_t1(i):
            j, tb = units[i]
            b = bufs(i)
            add("pool", lambda e, b=b, tb=tb: e.tensor_tensor(out=fsk[:, b["t1"], :], in0=sbk[:, b["qn"], :], in1=ctab[:, tb * TB:(tb + 1) * TB],
                                                              op=ALU.mult),
                reads=[("sbk", b["qn"]), "ctab"], writes=[("fsk", b["t1"])])

        def e_t2(i):
            j, tb = units[i]
            b = bufs(i)
            Cb = psb[b["C"]]
            add("dve", lambda e, Cb=Cb, b=b, tb=tb: e.tensor_tensor(out=fsk[:, b["t2"], :], in0=Cb, in1=stab[:, tb * TB:(tb + 1) * TB],
                                                                    op=ALU.mult),
                reads=[("ps", b["C"]), "stab"], writes=[("fsk", b["t2"])])

        def e_add(i):
            j, tb = units[i]
            b = bufs(i)
            dd = (1, 4, 16)[g]
            T = (qT if j < 4 else kT)
            if dd == 1:
                dst = T[:, j % 4, tb * TB:(tb + 1) * TB]
                s1v, s2v = fsk[:, b["t1"], :], fsk[:, b["t2"], :]
            else:
                w = TB // dd
                dst = T[:, j % 4, :].rearrange("p (r m) -> p r m", r=dd)[:, :, tb * w:(tb + 1) * w]
                s1v = fsk[:, b["t1"], :].rearrange("p (m r) -> p r m", r=dd)
                s2v = fsk[:, b["t2"], :].rearrange("p (m r) -> p r m", r=dd)
            dkey = ("qT" if j < 4 else "kT", j % 4)
            add("dve" if i % 2 == 0 else "pool", lambda e, dst=dst, s1v=s1v, s2v=s2v: e.tensor_tensor(out=dst, in0=s1v, in1=s2v, op=ALU.add),
                reads=[("fsk", b["t1"]), ("fsk", b["t2"])], writes=[dkey])

        def ok(k):
            return 0 <= k < n

        for i in range(n + 6):
            if ok(i - 2):
                e_ss(i - 2)
            if ok(i - 4):
                e_rq(i - 4)
            if ok(i):
                e_proj(i)
            if ok(i - 2):
                e_sqrt(i - 2)
            if ok(i - 1):
                e_sq(i - 1)
            if ok(i - 2):
                e_recip(i - 2)
            if ok(i - 4):
                e_t2(i - 4)
                e_t1(i - 4)
            if ok(i - 3):
                e_qn(i - 3)
            if ok(i - 5):
                e_add(i - 5)

    def key_jobs(g):
        d = (1, 4, 16)[g]
        L = S // d
        jobs = []
        if g < 2:
            nqt = L // 128
            for r in range(d):
                for b in range(nqt + 1):
                    if b == 0:
                        jobs.append(dict(k0=r, kstep=d, kp=r * L, mk=64, mask=M_FIRST, segs=[(r, 0, True, False)]))
                    elif b == nqt:
                        jobs.append(dict(k0=(L - 64) * d + r, kstep=d, kp=r * L + L - 64, mk=64, mask=M_LAST,
                                         segs=[(r, nqt - 1, False, True)]))
                    else:
                        jobs.append(dict(k0=(128 * b - 64) * d + r, kstep=d, kp=r * L + 128 * b - 64, mk=128, mask=M_MID,
                                         segs=[(r, b - 1, False, True), (r, b, True, False)]))
        else:
            for r in range(d):
                jobs.append(dict(k0=r, kstep=d, kp=r * L, mk=128, mask=M_G2, segs=[(r, 0, True, True)]))
        return jobs, d

    qt_par = {}
    qt_counter = [0]
    V_BANK, ST_PAIRS, NUM_BANKS, DEN_BANK = 4, (5, 0), (7, 3), 2
    pj_count = [0]
    vt_count = [0]

    def attention(g):
        jobs, d = key_jobs(g)
        import os
        if os.environ.get("ATT_JOBS"):
            jobs = jobs[:int(os.environ["ATT_JOBS"])]
        items = []
        for ji in range(len(jobs)):
            for pr in range(4):
                items.append((ji, pr))
        jinfo = {}

        def vproj(ji):
            job = jobs[ji]
            mk = job["mk"]
            ktok = slice(job["k0"], job["k0"] + (mk - 1) * job["kstep"] + 1, job["kstep"])
            vs = vt_count[0] % 2
            vt_count[0] += 1
            for kc in range(8):
                add("pe", lambda e, kc=kc, ktok=ktok, mk=mk: e.matmul(psb[V_BANK][0:mk, :], lhsT=hT[:, kc, ktok], rhs=wv_view[:, kc, :],
                                                                      start=(kc == 0), stop=(kc == 7)),
                    reads=[("hT", kc, 0), ("hT", kc, 1), ("hT", kc, 2), ("hT", kc, 3), ("w", 8), ("w", 9), ("w", 10), ("w", 11)], writes=[("ps", V_BANK)])
            add("act", lambda e, vs=vs, mk=mk: e.activation(out=vtile[0:mk, vs, :], in_=psb[V_BANK][0:mk, :], func=AF.Copy),
                reads=[("ps", V_BANK)], writes=[("vt", vs)])
            seginfo = []
            for (r, qt, is_start, is_stop) in job["segs"]:
                key = (g, r, qt)
                if is_start:
                    qt_par[key] = qt_counter[0] % 2
                    qt_counter[0] += 1
                par = qt_par[key]
                q0 = (128 * qt) * d + r
                qtok = slice(q0, q0 + 127 * d + 1, d)
                seginfo.append((par, qtok, is_start, is_stop))
            L_ = S // d
            r0_, qt0_ = job["segs"][0][0], job["segs"][0][1]
            qp0 = r0_ * L_ + 128 * qt0_
            kperm = slice(job["kp"], job["kp"] + mk)
            qperm = slice(qp0, qp0 + 128 * len(job["segs"]))
            jinfo[ji] = (mk, ktok, vs, seginfo, kperm, qperm)

        def stage_a(ji, pr, slot):
            mk, ktok, vs, seginfo, kperm, qperm = jinfo[ji]
            nq = 128 * len(seginfo)
            b0 = ST_PAIRS[slot]
            mcol = jobs[ji]["mask"]
            for hh in range(2):
                ST = psb[b0 + hh]
                if mk == 128:
                    add("pe", lambda e, ST=ST, nq=nq, mcol=mcol: e.matmul(ST[:, 0:nq], lhsT=ident, rhs=cm[:, mcol:mcol + nq],
                                                                          start=True, stop=False),
                        reads=["cm"], writes=[("ps", b0 + hh)])
                else:
                    r0 = 64 * hh
                    add("pe", lambda e, ST=ST, nq=nq, mcol=mcol, r0=r0: e.matmul(ST[0:64, 0:nq], lhsT=ident[r0:r0 + 64, r0:r0 + 64],
                                                                                 rhs=cm[r0:r0 + 64, mcol:mcol + nq], start=True, stop=False),
                        reads=["cm"], writes=[("ps", b0 + hh)])
            for hh in range(2):
                ST = psb[b0 + hh]
                add("pe", lambda e, ST=ST, hh=hh, pr=pr, kperm=kperm, qperm=qperm, nq=nq, mk=mk: e.matmul(
                    ST[0:mk, 0:nq], lhsT=kT[64 * hh:64 * hh + 64, pr, kperm], rhs=qT[64 * hh:64 * hh + 64, pr, qperm],
                    start=False, stop=True),
                    reads=[("kT", pr), ("qT", pr)], writes=[("ps", b0 + hh)])
            stp = ps_all[:, b0 * 512:(b0 + 2) * 512].rearrange("p (h n) -> p h n", h=2)
            pv = pT2[:, slot, :].rearrange("p (h n) -> p h n", h=2)
            add("act", lambda e, stp=stp, pv=pv, mk=mk, nq=nq: e.activation(out=pv[0:mk, :, 0:nq], in_=stp[0:mk, :, 0:nq], func=AF.Exp, scale=0.125),
                reads=[("ps", b0), ("ps", b0 + 1)], writes=[("sbk", 6 + slot)])

        def stage_b(ji, pr, slot):
            mk, ktok, vs, seginfo, kperm, qperm = jinfo[ji]
            for si, (par, qtok, is_start, is_stop) in enumerate(seginfo):
                for hh in range(2):
                    h = 2 * pr + hh
                    col = hh * 256 + si * 128
                    nb = NUM_BANKS[par]
                    add("pe", lambda e, nb=nb, hh=hh, pr=pr, h=h, vs=vs, mk=mk, slot=slot, col=col, is_start=is_start, is_stop=is_stop:
                        e.matmul(psb[nb][64 * hh:64 * hh + 64, pr * 128:(pr + 1) * 128], lhsT=vtile[0:mk, vs, h * 64:(h + 1) * 64],
                                 rhs=pT2[0:mk, slot, col:col + 128], start=(is_start and pr == 0), stop=(is_stop and pr == 3),
                                 tile_position=(0, 64 * hh)),
                        reads=[("vt", vs), ("sbk", 6 + slot)], writes=[("ps", nb)])
            for hh in range(2):
                h = 2 * pr + hh
                for si, (par, qtok, is_start, is_stop) in enumerate(seginfo):
                    col = hh * 256 + si * 128
                    add("pe", lambda e, par=par, h=h, mk=mk, slot=slot, col=col, is_start=is_start, is_stop=is_stop:
                        e.matmul(psb[DEN_BANK][32 * par:32 * par + 8, 0:128], lhsT=sel[0:mk, h, :],
                                 rhs=pT2[0:mk, slot, col:col + 128], start=(is_start and h == 0), stop=(is_stop and h == 7),
                                 tile_position=(0, 32 * par)),
                        reads=["cm", ("sbk", 6 + slot)], writes=[("psden", par)])
            if pr == 3:
                for (par, qtok, is_start, is_stop) in seginfo:
                    if is_stop:
                        nb = NUM_BANKS[par]
                        add("dve", lambda e, nb=nb, qtok=qtok: e.tensor_tensor(out=acc[:, :, qtok],
                                                                               in0=psb[nb].rearrange("p (a t) -> p a t", a=4),
                                                                               in1=acc[:, :, qtok], op=ALU.add),
                            reads=[("ps", nb), "acc"], writes=["acc"])
                        add("dve", lambda e, par=par, qtok=qtok: e.tensor_tensor(out=den[32 * par:32 * par + 8, qtok],
                                                                                 in0=psb[DEN_BANK][32 * par:32 * par + 8, 0:128],
                                                                                 in1=den[32 * par:32 * par + 8, qtok], op=ALU.add),
                            reads=[("psden", par), "den"], writes=["den"])

        n = len(items)
        for i in range(n + 1):
            if i < n:
                ji, pr = items[i]
                if pr == 0:
                    vproj(ji)
                stage_a(ji, pr, i % 2)
            if i >= 1:
                ji, pr = items[i - 1]
                stage_b(ji, pr, (i - 1) % 2)

    for g in range(3):
        if g > 0:
            load_qkv(g)
        qk_units(g)
        if g == 0:
            add("pool", lambda e: e.memset(acc, 0.0), writes=["acc"] + [("xall", c) for c in range(4)])
            chk("Q0", qT)
            chk("K0", kT)
        attention(g)
        if g == 0:
            chk("A0", acc)
            chk("D0", den)
    chk("A2", acc)
    chk("D2", den)

    for cc in range(4):
        load_w(cc, w_in_c[36 + cc])
        load_w(4 + cc, w_in_c[40 + cc])
    UW_ = 2080
    R2_OLD = [("qT", i_) for i_ in range(4)] + [("kT", i_) for i_ in range(4)] + [("vt", 0), ("vt", 1)]
    add("pool", lambda e: e.memset(uT[:, :, 0:15], 0.0), writes=["uT"] + R2_OLD)
    add("pool", lambda e: e.memset(uT[:, :, 15 + S:UW_], 0.0), writes=["uT"])
    def build_diag(cc, dg):
        idb = ident.rearrange("p (o m) -> p o m", o=1).broadcast_to([128, 31, 128])
        cwb = vecs[:, V_CW + cc * 31:V_CW + cc * 31 + 31].rearrange("p (j o) -> p j o", o=1).broadcast_to([128, 31, 128])
        add("dve", lambda e: e.tensor_tensor(out=dg[:, cc % 2, :, :], in0=idb, in1=cwb, op=ALU.mult),
            reads=["cm", "vecs"], writes=[("diag", cc)] + (["acc"] if cc >= 2 else R2_OLD))


    selb = vecs[0:40, V_SELB:V_SELB + 512].rearrange("p (a m) -> p a m", a=4)
    rdb = sp_f.rearrange("p (a t) -> p a t", a=4)
    p4 = [(pr, tb) for pr in range(4) for tb in range(NT)]

    def p4_mm(k):
        pr, tb = p4[k]
        bank = 4 + k % 4
        add("pe", lambda e: e.matmul(psb[bank], lhsT=selb[:, pr, :], rhs=den[0:40, tb * TB:(tb + 1) * TB], start=True, stop=True),
            reads=["den", "vecs"], writes=[("ps", bank)])

    def p4_ln(k):
        bank = 4 + k % 4
        add("act", lambda e: e.activation(out=rdb[:, k % 4, :], in_=psb[bank], func=AF.Ln),
            reads=[("ps", bank)], writes=[("rdb", k % 4)])

    def p4_exp(k):
        add("act", lambda e: e.activation(out=rdb[:, k % 4, :], in_=rdb[:, k % 4, :], func=AF.Exp, scale=-1.0),
            reads=[("rdb", k % 4)], writes=[("rdb", k % 4)])

    def p4_mul(k):
        pr, tb = p4[k]
        add("dve", lambda e: e.tensor_tensor(out=attnT[:, pr, tb * TB:(tb + 1) * TB], in0=rdb[:, k % 4, :],
                                             in1=acc[:, pr, tb * TB:(tb + 1) * TB], op=ALU.mult),
            reads=[("rdb", k % 4), "acc"], writes=[("attnT", pr)])

    def p4_step(k):
        if k < len(p4):
            p4_mm(k)
        if 0 <= k - 1 < len(p4):
            p4_exp(k - 1)
        if k < len(p4):
            p4_ln(k)
        if 0 <= k - 2 < len(p4):
            p4_mul(k - 2)

    u = 0
    rs2 = fs[:, 0:1024].rearrange("p (a t) -> p a t", a=2)
    for cc in range(4):
        for tb in range(NT):
            par = u % 2
            u += 1
            A, B = psb[par], psb[2 + par]
            for kc in range(8):
                add("pe", lambda e, cc=cc, kc=kc, tb=tb, A=A: e.matmul(A, lhsT=wsl[:, cc, kc, :], rhs=hT[:, kc, tb * TB:(tb + 1) * TB],
                                                                       start=(kc == 0), stop=(kc == 7)),
                    reads=[("w", cc), ("hT", kc, tb)], writes=[("ps", par)])
            for kc in range(8):
                add("pe", lambda e, cc=cc, kc=kc, tb=tb, B=B: e.matmul(B, lhsT=wsl[:, 4 + cc, kc, :], rhs=hT[:, kc, tb * TB:(tb + 1) * TB],
                                                                       start=(kc == 0), stop=(kc == 7)),
                    reads=[("w", 4 + cc), ("hT", kc, tb)], writes=[("ps", 2 + par)])
            add("act", lambda e, B=B, par=par: e.activation(out=rs2[:, par, :], in_=B, func=AF.Sigmoid),
                reads=[("ps", 2 + par)], writes=[("rs2", par)])
            add("dve", lambda e, A=A, par=par, cc=cc, tb=tb: e.tensor_tensor(out=uT[:, cc, 15 + tb * TB:15 + (tb + 1) * TB], in0=A,
                                                                             in1=rs2[:, par, :], op=ALU.mult),
                reads=[("ps", par), ("rs2", par)], writes=["uT"])
            if u == 3:
                build_diag(0, diag01)
            if u == 6:
                build_diag(1, diag01)
            if u % 4 == 0:
                for k_ in range(u - 4, u):
                    p4_step(k_)
    p4_step(16)
    p4_step(17)
    chk("P4", attnT)
    diag23 = bv(A_ACC, 2 * 31 * 128).rearrange("p (c j m) -> p c j m", c=2, j=31)
    for cc in (2, 3):
        build_diag(cc, diag23)
    def load_p5(c):
        if c >= 8:
            return
        s0 = (c % 4) * 3
        oa_v = bv(A_W + s0 * 512, 512).rearrange("p (k m) -> p k m", k=4)
        pw_v = bv(A_W + s0 * 512 + 256, 512).rearrange("p (k m) -> p k m", k=4)
        add("pool", lambda e, oa_v=oa_v, c=c: e.dma_start(out=oa_v, in_=w_oa[c].rearrange("p (k m) -> p k m", k=4)),
            writes=[("w", s0)], dma_sem="w%d" % s0)
        add("pool", lambda e, pw_v=pw_v, c=c: e.dma_start(out=pw_v, in_=w_pw[c].rearrange("p (k m) -> p k m", k=4)),
            writes=[("w", s0)], dma_sem="w%d" % s0)
        load_w(s0 + 1, w_in_c[44 + c])
        load_w(s0 + 2, w_in_c[52 + c])

    load_p5(0)
    load_p5(1)
    load_p5(2)
    ycv2 = fs.rearrange("p (b a t) -> p b a t", b=2, a=4)
    ysq = sb[:, 0:2048].rearrange("p (a t) -> p a t", a=4)
    ybf = sb[:, 2048:4096].rearrange("p (a t) -> p a t", a=4)
    mean2 = sp_f[:, 0:1024].rearrange("p (a t) -> p a t", a=2)
    rstd2 = sp_f[:, 1024:2048].rearrange("p (a t) -> p a t", a=2)

    def conv_cc(tb, cc):
        yb = tb % 2
        dg = diag01 if cc < 2 else diag23
        bank = cc % 2
        sbank = 2 + 2 * yb
        for j in range(31):
            add("pe", lambda e, dg=dg, cc=cc, j=j, tb=tb, bank=bank: e.matmul(psb[bank], lhsT=dg[:, cc % 2, j, :],
                                                                              rhs=uT[:, cc, tb * TB + j:tb * TB + j + TB],
                                                                              start=(j == 0), stop=(j == 30)),
                reads=[("diag", cc), "uT"], writes=[("ps", bank)])
        add("act", lambda e, cc=cc, bank=bank, yb=yb: e.activation(out=ycv2[:, yb, cc, :], in_=psb[bank], func=AF.Identity,
                                                                   bias=vecs[:, V_CB + cc:V_CB + cc + 1], scale=1.0),
            reads=[("ps", bank), "vecs"], writes=[("ycv", yb, cc)])
        add("act", lambda e, cc=cc, yb=yb: e.activation(out=ysq[:, cc, :], in_=ycv2[:, yb, cc, :], func=AF.Square),
            reads=[("ycv", yb, cc)], writes=[("ysq", cc)])
        add("pool", lambda e, cc=cc, yb=yb: e.tensor_copy(out=ybf[:, cc, :], in_=ycv2[:, yb, cc, :]),
            reads=[("ycv", yb, cc)], writes=[("ybf", cc)])

    def conv_stats(tb, cc):
        yb = tb % 2
        sbank = 2 + 2 * yb
        add("pe", lambda e, cc=cc, sbank=sbank: e.matmul(psb[sbank], lhsT=ones_m, rhs=ybf[:, cc, :], start=(cc == 0), stop=(cc == 3)),
            reads=[("ybf", cc), "cm"], writes=[("ps", sbank)])
        add("pe", lambda e, cc=cc, sbank=sbank: e.matmul(psb[sbank + 1], lhsT=ones_m, rhs=ysq[:, cc, :], start=(cc == 0), stop=(cc == 3)),
            reads=[("ysq", cc), "cm"], writes=[("ps", sbank + 1)])

    def ln_stats(tb):
        yb = tb % 2
        sbank = 2 + 2 * yb
        mean_t, var_t = mean2[:, yb, :], rstd2[:, yb, :]
        add("dve", lambda e: e.tensor_scalar(out=mean_t, in0=psb[sbank], scalar1=1.0 / 512, scalar2=None, op0=ALU.mult),
            reads=[("ps", sbank)], writes=[("mean", yb)])
        add("dve", lambda e: e.tensor_tensor(out=var_t, in0=mean_t, in1=mean_t, op=ALU.mult), reads=[("mean", yb)], writes=[("var", yb)])
        add("dve", lambda e: e.scalar_tensor_tensor(out=var_t, in0=psb[sbank + 1], scalar=1.0 / 512, in1=var_t, op0=ALU.mult, op1=ALU.subtract),
            reads=[("ps", sbank + 1), ("var", yb)], writes=[("var", yb)])
        add("act", lambda e: e.activation(out=var_t, in_=var_t, func=AF.Ln, bias=cf[:, 0:1], scale=1.0),
            reads=[("var", yb), "cf"], writes=[("var", yb)])
        add("act", lambda e: e.activation(out=var_t, in_=var_t, func=AF.Exp, scale=-0.5), reads=[("var", yb)], writes=[("var", yb)])

    def ln_tail(tb, cc):
        yb = tb % 2
        y = ycv2[:, yb, cc, :]
        add("dve", lambda e: e.tensor_tensor(out=y, in0=y, in1=mean2[:, yb, :], op=ALU.subtract),
            reads=[("ycv", yb, cc), ("mean", yb)], writes=[("ycv", yb, cc)])
        add("pool", lambda e: e.tensor_tensor(out=y, in0=y, in1=rstd2[:, yb, :], op=ALU.mult),
            reads=[("ycv", yb, cc), ("var", yb)], writes=[("ycv", yb, cc)])
        add("act", lambda e: e.activation(out=ucT[:, cc, tb * TB:(tb + 1) * TB], in_=y, func=AF.Silu,
                                          bias=vecs[:, V_LB + cc:V_LB + cc + 1], scale=vecs[:, V_LW + cc:V_LW + cc + 1]),
            reads=[("ycv", yb, cc), "vecs"], writes=[("ucT", cc, tb)])

    steps = [(tb, cc) for tb in range(NT) for cc in range(4)]
    for k in range(len(steps) + 6):
        if k - 6 >= 0 and k - 6 < len(steps):
            ln_tail(*steps[k - 6])
        if k < len(steps):
            conv_cc(*steps[k])
        if 0 <= k - 1 < len(steps):
            tb1, cc1 = steps[k - 1]
            conv_stats(tb1, cc1)
            if cc1 == 3:
                ln_stats(tb1)

    chk("P3", ucT)

    ga_t = fs[:, 0:1024].rearrange("p (a t) -> p a t", a=2)
    gb_t = fs[:, 1024:2048].rearrange("p (a t) -> p a t", a=2)
    ta_t = fv(A_ACC + 4096, 1024).rearrange("p (a t) -> p a t", a=2)
    tb_t = fv(A_ACC + 5120, 1024).rearrange("p (a t) -> p a t", a=2)
    u = 0
    for c in range(8):
        s0 = (c % 4) * 3
        oa_v = bv(A_W + s0 * 512, 512).rearrange("p (k m) -> p k m", k=4)
        pw_v = bv(A_W + s0 * 512 + 256, 512).rearrange("p (k m) -> p k m", k=4)
        load_p5(c + 3)
        if c == 6:
            for c6 in range(3):
                load_w(c6, w_out_c[c6])
        if c == 7:
            load_w(3, w_out_c[3])
        for tb in range(NT):
            par = u % 2
            u += 1
            YA, YB, GA, GB = psb[4 * par], psb[4 * par + 1], psb[4 * par + 2], psb[4 * par + 3]
            tok = slice(tb * TB, (tb + 1) * TB)
            for kc in range(4):
                add("pe", lambda e, kc=kc, tok=tok, YA=YA, oa_v=oa_v: e.matmul(YA, lhsT=oa_v[:, kc, :], rhs=attnT[:, kc, tok],
                                                                               start=(kc == 0), stop=(kc == 3)),
                    reads=[("w", s0), ("attnT", kc)], writes=[("ps", 4 * par)])
            for kc in range(4):
                add("pe", lambda e, kc=kc, tok=tok, YB=YB, pw_v=pw_v: e.matmul(YB, lhsT=pw_v[:, kc, :], rhs=ucT[:, kc, tok],
                                                                               start=(kc == 0), stop=(kc == 3)),
                    reads=[("w", s0), ("ucT", kc, tb)], writes=[("ps", 4 * par + 1)])
            for kc in range(8):
                add("pe", lambda e, kc=kc, tok=tok, GA=GA, s0=s0: e.matmul(GA, lhsT=wsl[:, s0 + 1, kc, :], rhs=hT[:, kc, tok],
                                                                           start=(kc == 0), stop=(kc == 7)),
                    reads=[("w", s0 + 1), ("hT", kc, tb)], writes=[("ps", 4 * par + 2)])
            for kc in range(8):
                add("pe", lambda e, kc=kc, tok=tok, GB=GB, s0=s0: e.matmul(GB, lhsT=wsl[:, s0 + 2, kc, :], rhs=hT[:, kc, tok],
                                                                           start=(kc == 0), stop=(kc == 7)),
                    reads=[("w", s0 + 2), ("hT", kc, tb)], writes=[("ps", 4 * par + 3)])
            add("act", lambda e, GA=GA, par=par, c=c: e.activation(out=ga_t[:, par, :], in_=GA, func=AF.Sigmoid,
                                                                   bias=vecs[:, V_BGA + c:V_BGA + c + 1], scale=1.0),
                reads=[("ps", 4 * par + 2), "vecs"], writes=[("ga", par), ("ycv", 0, 0), ("ycv", 0, 1)])
            add("act", lambda e, GB=GB, par=par, c=c: e.activation(out=gb_t[:, par, :], in_=GB, func=AF.Sigmoid,
                                                                   bias=vecs[:, V_BGB + c:V_BGB + c + 1], scale=1.0),
                reads=[("ps", 4 * par + 3), "vecs"], writes=[("gb", par), ("ycv", 0, 2), ("ycv", 0, 3)])
            add("dve", lambda e, YA=YA, par=par: e.tensor_tensor(out=ta_t[:, par, :], in0=YA, in1=ga_t[:, par, :], op=ALU.mult),
                reads=[("ps", 4 * par), ("ga", par)], writes=[("ta", par)])
            add("dve", lambda e, YB=YB, par=par: e.tensor_tensor(out=tb_t[:, par, :], in0=YB, in1=gb_t[:, par, :], op=ALU.mult),
                reads=[("ps", 4 * par + 1), ("gb", par)], writes=[("tb", par)])
            add("pool", lambda e, par=par, c=c, tok=tok: e.tensor_tensor(out=zT[:, c, tok], in0=ta_t[:, par, :], in1=tb_t[:, par, :], op=ALU.add),
                reads=[("ta", par), ("tb", par)], writes=[("zT", c), "uT", ("diag", 0), ("diag", 1)])

    chk("P5", zT)
    barrier()
    xst6 = fv(A_R3, 4096).rearrange("p (a t) -> p a t", a=2)
    rstd7 = fv(A_R3 + 4096, 2048)
    u = 0
    for c in range(9):
        if c < 8:
            slot = c % 4
            if c >= 4:
                load_w(slot, w_out_c[c])
            xs = c % 2
            add("sp", lambda e, c=c, xs=xs: e.dma_start(out=xst6[:, xs, :], in_=xT[c * 128:(c + 1) * 128, :]),
                writes=[("xst6", xs)], dma_sem="xst%d" % xs)
            for tb in range(NT):
                bank = u % 4
                u += 1
                tok = slice(tb * TB, (tb + 1) * TB)
                for kc in range(8):
                    add("pe", lambda e, kc=kc, tok=tok, bank=bank, slot=slot: e.matmul(psb[bank], lhsT=wsl[:, slot, kc, :], rhs=zT[:, kc, tok],
                                                                                       start=(kc == 0), stop=(kc == 7)),
                        reads=[("w", slot), ("zT", kc)], writes=[("ps", bank)])
                add("dve", lambda e, c=c, tok=tok, bank=bank, xs=xs: e.tensor_tensor(out=x1T[:, c, tok], in0=psb[bank], in1=xst6[:, xs, tok], op=ALU.add),
                    reads=[("ps", bank), ("xst6", xs)], writes=[("x1T", c)])
        if c >= 1:
            rms_pass1(c - 1, x1T[:, c - 1, :], ("x1T", c - 1), 4)
    chk("P6", x1T)
    rms_rstd(rstd7, 4)
    rms_pass2_tb([x1T[:, c, :] for c in range(8)], [("x1T", c) for c in range(8)], hT, V_N2W, "h2T", rstd7)

    chk("P7", hT)
    sg_t = fv(A_R3 + 6144, 1024).rearrange("p (a t) -> p a t", a=2)
    groups = [list(range(0, 8)), list(range(8, 16)), list(range(16, 22))]
    u = 0
    fcount = 0
    out_ops = []
    all_f = [f for grp in groups for f in grp]

    def load_fi(k):
        if k < len(all_f):
            sgk = 8 + 2 * (k % 2)
            load_w(sgk, w_fi[all_f[k]])
            load_w(sgk + 1, w_fi[22 + all_f[k]])

    load_fi(0)

    def ffn_unit(fi, sg_, tb):
        par = ucnt[0] % 2
        ucnt[0] += 1
        tok = slice(tb * TB, (tb + 1) * TB)
        GT, UP = psb[2 * par], psb[2 * par + 1]
        for kc in range(8):
            add("pe", lambda e, kc=kc: e.matmul(GT, lhsT=wsl[:, sg_, kc, :], rhs=hT[:, kc, tok], start=(kc == 0), stop=(kc == 7)),
                reads=[("w", sg_), ("h2T", kc, tb)], writes=[("ps", 2 * par)])
        for kc in range(8):
            add("pe", lambda e, kc=kc: e.matmul(UP, lhsT=wsl[:, sg_ + 1, kc, :], rhs=hT[:, kc, tok], start=(kc == 0), stop=(kc == 7)),
                reads=[("w", sg_ + 1), ("h2T", kc, tb)], writes=[("ps", 2 * par + 1)])
        add("act", lambda e: e.activation(out=sg_t[:, par, :], in_=GT, func=AF.Silu),
            reads=[("ps", 2 * par)], writes=[("sg", par)])
        add("dve", lambda e: e.tensor_tensor(out=aT[:, fi, tok], in0=UP, in1=sg_t[:, par, :], op=ALU.mult),
            reads=[("ps", 2 * par + 1), ("sg", par)], writes=[("aT", fi)])

    ucnt = [u]
    for gi, grp in enumerate(groups):
        ucnt[0] = u
        fi_start = 0
        if gi == 0:
            load_fi(1)
            for tb in range(NT):
                ffn_unit(0, 8, tb)
                ffn_unit(1, 10, tb)
            fcount = 2
            load_fi(2)
            for fj, f2 in enumerate(grp):
                load_w(fj, w_fo[f2])
            fi_start = 2
        for fi in range(fi_start, len(grp)):
            sg_ = 8 + 2 * (fcount % 2)
            fcount += 1
            load_fi(fcount)
            if fi == 1:
                for fj, f2 in enumerate(grp):
                    load_w(fj, w_fo[f2])
            for tb in range(NT):
                ffn_unit(fi, sg_, tb)
        u = ucnt[0]
        last = gi == len(groups) - 1
        for c in range(8):
            for tb in range(NT):
                bank = 4 + (u % 4)
                u += 1
                tok = slice(tb * TB, (tb + 1) * TB)
                for fi in range(len(grp)):
                    add("pe", lambda e, fi=fi, c=c, tok=tok, bank=bank, n=len(grp): e.matmul(
                        psb[bank], lhsT=bv(A_W + fi * 512, 1024)[:, c * 128:(c + 1) * 128], rhs=aT[:, fi, tok],
                        start=(fi == 0), stop=(fi == n - 1)),
                        reads=[("w", fi), ("aT", fi)], writes=[("ps", bank)])
                add("dve", lambda e, c=c, tok=tok, bank=bank: e.tensor_tensor(out=x1T[:, c, tok], in0=psb[bank], in1=x1T[:, c, tok], op=ALU.add),
                    reads=[("ps", bank), ("x1T", c)], writes=[("x1T", c)] + ([("x1o", c, tb)] if last else []))
                if last:
                    out_ops.append(add("sp", lambda e, c=c, tok=tok: e.dma_start(out=outT[c * 128:(c + 1) * 128, tok], in_=x1T[:, c, tok]),
                                       reads=[("x1o", c, tb)], writes=[("out", c, tb)], dma_sem="out"))


def _emit(L):
    nc, sc, es = L["nc"], L["sc"], L["es"]
    sc.finalize()
    sem_names = ["pe", "act", "dve", "pool", "sp"] + sorted({op.sem_key for op in sc.ops if op.is_dma})
    sems = {n: es.enter_context(nc.semaphore("s_" + n)) for n in sem_names}
    out_total = sc.totals.get("out", 0)

    def emitter(engname):
        def f(e):
            for op in sc.ops:
                if op.eng != engname:
                    continue
                for (sn, v) in op.waits:
                    e.wait_ge(sems[sn], v)
                if op.fn is None:
                    continue
                ins = op.fn(e)
                if op.is_dma:
                    ins.then_inc(sems[op.sem_key], 16)
                elif op.signal:
                    ins.then_inc(sems[op.eng], 1)
            if engname == "sp":
                e.wait_ge(sems["out"], out_total)
        return f

    with nc.Block() as block:
        block.sync(emitter("sp"))
        block.tensor(emitter("pe"))
        block.scalar(emitter("act"))
        block.vector(emitter("dve"))
        block.gpsimd(emitter("pool"))


def _consts():
    cmh = np.zeros((128, NCM), np.float32)
    cmh[:, M_ID:M_ID + 128] = np.eye(128, dtype=np.float32)
    bd = np.zeros((128, 128), np.float32)
    bd[:64, :64] = 1.0
    bd[64:, 64:] = 1.0
    cmh[:, M_BD:M_BD + 128] = bd
    cmh[:, M_ONES:M_ONES + 128] = 1.0
    R = np.zeros((128, 128), np.float32)
    for base in (0, 64):
        for i in range(8):
            R[base + i + 8, base + i] = -1.0
            R[base + i, base + i + 8] = 1.0
    cmh[:, M_ROT:M_ROT + 128] = R
    i = np.arange(128)[:, None]
    jq = np.arange(256)[None, :]
    mid = np.where((jq >= i) & (jq <= i + 128), 0.0, NEG).astype(np.float32)
    cmh[:, M_MID:M_MID + 256] = mid
    cmh[:, M_MID + 256:M_MID + 512] = mid
    j1 = np.arange(128)[None, :]
    first = np.where(np.abs(j1 - i) <= 64, 0.0, NEG).astype(np.float32)
    first[64:] = first[:64]
    cmh[:, M_FIRST:M_FIRST + 128] = first
    cmh[:, M_FIRST + 128:M_FIRST + 256] = first
    last = np.where(j1 >= i, 0.0, NEG).astype(np.float32)
    last[64:] = last[:64]
    cmh[:, M_LAST:M_LAST + 128] = last
    cmh[:, M_LAST + 128:M_LAST + 256] = last
    g2 = np.where(np.abs(j1 - i) <= 64, 0.0, NEG).astype(np.float32)
    cmh[:, M_G2:M_G2 + 128] = g2
    cmh[:, M_G2 + 128:M_G2 + 256] = g2
    selm = np.zeros((128, 8, 8), np.float32)
    for h in range(8):
        selm[:, h, h] = 1.0
    cmh[:, M_SEL:M_SEL + 64] = selm.reshape(128, 64)
    return cmh


_NC_CACHE = {}


def kernel(x, positions, norm1_w, w_in, b_gate, q_norm_w, k_norm_w, w_o_attn, conv_w, conv_b, conv_ln_w,
           conv_ln_b, w_pw_conv, w_out, norm2_w, w_ffn_in, w_ffn_out, _prep_only=False):
    f32 = np.float32
    x = np.asarray(x, f32)
    positions = np.asarray(positions, np.int32)
    B = x.shape[0]
    l = 0
    vec = np.zeros((128, NV), f32)
    vec[:, V_N1W:V_N1W + 8] = np.asarray(norm1_w, f32)[l].reshape(8, 128).T
    vec[:, V_N2W:V_N2W + 8] = np.asarray(norm2_w, f32)[l].reshape(8, 128).T
    vec[:, V_QW] = np.tile(np.asarray(q_norm_w, f32)[l], 2)
    vec[:, V_KW] = np.tile(np.asarray(k_norm_w, f32)[l], 2)
    bg = np.asarray(b_gate, f32)[l]
    vec[:, V_BGA:V_BGA + 8] = bg[0].reshape(8, 128).T
    vec[:, V_BGB:V_BGB + 8] = bg[1].reshape(8, 128).T
    vec[:, V_CB:V_CB + 4] = np.asarray(conv_b, f32)[l].reshape(4, 128).T
    vec[:, V_LW:V_LW + 4] = np.asarray(conv_ln_w, f32)[l].reshape(4, 128).T
    vec[:, V_LB:V_LB + 4] = np.asarray(conv_ln_b, f32)[l].reshape(4, 128).T
    invf = np.zeros(128, f32)
    invf_lo = np.zeros(128, f32)
    inv_freq64 = 500000.0 ** (-np.arange(0, 16, 2, dtype=np.float64) / 16.0)
    inv_hi = inv_freq64.astype(f32)
    inv_lo = (inv_freq64 - inv_hi.astype(np.float64)).astype(f32)
    for base in (0, 64):
        invf[base:base + 8] = inv_hi
        invf[base + 8:base + 16] = inv_hi
        invf_lo[base:base + 8] = inv_lo
        invf_lo[base + 8:base + 16] = inv_lo
    vec[:, V_INVF] = invf
    vec[:, V_INVF + 1] = invf_lo
    cw = np.asarray(conv_w, f32)[l]
    vec[:, V_CW:V_CW + 124] = cw.T.reshape(4, 128, 31).transpose(1, 0, 2).reshape(128, 124)
    selb = np.zeros((8, 4, 128), f32)
    for pr in range(4):
        selb[2 * pr, pr, :64] = 1.0
        selb[2 * pr + 1, pr, 64:] = 1.0
    vec[0:8, V_SELB:V_SELB + 512] = selb.reshape(8, 512)
    vec[32:40, V_SELB:V_SELB + 512] = selb.reshape(8, 512)
    cmh = _consts()

    def chunk_lhsT(w):
        K, N = w.shape
        return np.ascontiguousarray(w.reshape(K // 128, 128, N // 128, 128).transpose(2, 1, 0, 3)).reshape(N // 128, 128, (K // 128) * 128)

    w_in0 = np.asarray(w_in, f32)[l]
    w_in_c = chunk_lhsT(w_in0)
    w_v = np.ascontiguousarray(w_in0[:, 3072:4608].reshape(8, 128, 3, 512).transpose(2, 1, 0, 3)).reshape(3, 128, 4096)
    w_oa = chunk_lhsT(np.asarray(w_o_attn, f32)[l])
    w_pw = chunk_lhsT(np.asarray(w_pw_conv, f32)[l])
    w_out_c = chunk_lhsT(np.asarray(w_out, f32)[l])
    w_fi = chunk_lhsT(np.asarray(w_ffn_in, f32)[l])
    w_fo = np.ascontiguousarray(np.asarray(w_ffn_out, f32)[l].reshape(22, 128, 1024))

    in_maps = []
    for b in range(B):
        in_maps.append({
            "xT": np.ascontiguousarray(x[b].T),
            "posr": np.ascontiguousarray(np.broadcast_to(positions[b][None, :], (128, S))),
            "vecs": vec, "cmats": cmh, "w_in_c": w_in_c, "w_v": w_v, "w_oa": w_oa, "w_pw": w_pw,
            "w_out_c": w_out_c, "w_fi": w_fi, "w_fo": w_fo,
        })
    if _prep_only:
        return in_maps
    if "nc" not in _NC_CACHE:
        _NC_CACHE["nc"] = build_program()
    nc = _NC_CACHE["nc"]
    res = run_bass_kernel_spmd(nc, in_maps, core_ids=list(range(B)))
    out = np.stack([np.ascontiguousarray(np.asarray(r["outT"], f32).T) for r in res.results], axis=0)
    return out.astype(f32)
```

```python
import numpy as np
import concourse.bass as bass
import concourse.mybir as mybir
from concourse.bass_utils import run_bass_kernel_spmd

F32 = mybir.dt.float32
BF16 = mybir.dt.bfloat16
I32 = mybir.dt.int32
ALU = mybir.AluOpType
AF = mybir.ActivationFunctionType

S = 2048
D = 1024
NT = 4
TB = 512
EPS = 1e-6
NEG = -30000.0
PI = float(np.pi)

V_N1W, V_N2W, V_QW, V_KW, V_BGA, V_BGB, V_CB, V_LW, V_LB, V_INVF, V_CW, V_SELB = 0, 8, 16, 17, 18, 26, 34, 38, 42, 46, 48, 172
NV = 172 + 512
M_ID, M_BD, M_ONES, M_ROT, M_MID, M_FIRST, M_LAST, M_G2, M_SEL = 0, 128, 256, 384, 512, 1024, 1280, 1536, 1792
NCM = 1792 + 64

A_ACC = 0
A_DEN = 8192
A_FS = 10240
A_SP = 14336
A_X1 = 0
A_HT = 16384
A_R2 = 24576
A_R3 = 33280
A_W = 41472
A_SB = 47616
A_END = 49664


class Op:
    __slots__ = ("eng", "fn", "deps", "signal", "val", "is_dma", "sem_key", "waits", "idx")


class Sched:
    def __init__(self):
        self.ops = []
        self.last_w = {}
        self.readers = {}
        self.barrier_op = None

    def add(self, eng, fn, reads=(), writes=(), dma_sem=None):
        op = Op()
        op.eng = eng
        op.fn = fn
        op.signal = False
        op.val = 0
        op.is_dma = dma_sem is not None
        op.sem_key = dma_sem
        op.idx = len(self.ops)
        writes = list(writes)
        if ("ps", 2) in writes:
            writes += [("psden", 0), ("psden", 1)]
        elif ("psden", 0) in writes or ("psden", 1) in writes:
            writes += [("ps", 2)]
        deps = {}
        if self.barrier_op is not None:
            deps[self.barrier_op.idx] = self.barrier_op
        for k in reads:
            w = self.last_w.get(k)
            if w is not None:
                deps[w.idx] = w
        for k in writes:
            w = self.last_w.get(k)
            if w is not None:
                deps[w.idx] = w
            for r in self.readers.get(k, {}).values():
                if isinstance(r, list):
                    for rr in r:
                        deps[rr.idx] = rr
                else:
                    deps[r.idx] = r
        out = []
        for d in deps.values():
            if d is op:
                continue
            if (not d.is_dma) and (not op.is_dma) and d.eng == "pe" and eng == "pe":
                continue
            out.append(d)
        op.deps = out
        for k in reads:
            rd = self.readers.setdefault(k, {})
            if op.is_dma:
                rd.setdefault("dma", []).append(op)
            else:
                rd[eng] = op
        for k in writes:
            self.last_w[k] = op
            self.readers[k] = {}
        self.ops.append(op)
        return op

    def finalize(self):
        for op in self.ops:
            for d in op.deps:
                d.signal = True
        cnt = {}
        for op in self.ops:
            if op.is_dma:
                cnt[op.sem_key] = cnt.get(op.sem_key, 0) + 16
                op.val = cnt[op.sem_key]
            elif op.signal:
                cnt[op.eng] = cnt.get(op.eng, 0) + 1
                op.val = cnt[op.eng]
        self.totals = cnt
        known = {}
        for op in self.ops:
            kn = known.setdefault(op.eng, {})
            w = {}
            for d in op.deps:
                sn = d.sem_key if d.is_dma else d.eng
                if d.val > w.get(sn, 0):
                    w[sn] = d.val
            op.waits = []
            for sn, v in w.items():
                if v > kn.get(sn, 0):
                    op.waits.append((sn, v))
                    kn[sn] = v


class _Stop(Exception):
    pass


def build_program(stop=None):
    nc = bass.Bass("TRN2", target_bir_lowering=False)
    dt = nc.dram_tensor
    xT = dt("xT", [D, S], F32, kind="ExternalInput").ap()
    posr = dt("posr", [128, S], I32, kind="ExternalInput").ap()
    vecs_d = dt("vecs", [128, NV], F32, kind="ExternalInput").ap()
    cm_d = dt("cmats", [128, NCM], F32, kind="ExternalInput").ap()
    w_in_c = dt("w_in_c", [60, 128, 1024], F32, kind="ExternalInput").ap()
    w_v = dt("w_v", [3, 128, 4096], F32, kind="ExternalInput").ap()
    w_oa = dt("w_oa", [8, 128, 512], F32, kind="ExternalInput").ap()
    w_pw = dt("w_pw", [8, 128, 512], F32, kind="ExternalInput").ap()
    w_out_c = dt("w_out_c", [8, 128, 1024], F32, kind="ExternalInput").ap()
    w_fi = dt("w_fi", [44, 128, 1024], F32, kind="ExternalInput").ap()
    w_fo = dt("w_fo", [22, 128, 1024], F32, kind="ExternalInput").ap()
    outT = dt("outT", [D, S], F32, kind="ExternalOutput").ap()

    from contextlib import ExitStack
    es = ExitStack()
    arena_t = es.enter_context(nc.sbuf_tensor("arena", [128, A_END], F32))
    vecs_t = es.enter_context(nc.sbuf_tensor("vecs_sb", [128, NV], F32))
    cm_t = es.enter_context(nc.sbuf_tensor("cm_sb", [128, NCM], BF16))
    cf_t = es.enter_context(nc.sbuf_tensor("cf_sb", [128, 8], F32))
    ps_t = es.enter_context(nc.psum_tensor("ps_all", [128, 4096], F32))
    arena = arena_t[:]
    vecs = vecs_t[:]
    cm = cm_t[:]
    cf = cf_t[:]
    ps_all = ps_t[:]
    psb = [ps_all[:, i * 512:(i + 1) * 512] for i in range(8)]

    def fv(off, n):
        return arena[:, off:off + n]

    def bv(off_words, n_elems):
        return arena[:, off_words:off_words + n_elems // 2].bitcast(BF16)

    acc = fv(A_ACC, 8192).rearrange("p (a t) -> p a t", a=4)
    den = fv(A_DEN, 2048)
    fs = fv(A_FS, 4096)
    sp_f = fv(A_SP, 2048)
    x1T = fv(A_X1, 16384).rearrange("p (a t) -> p a t", a=8)
    hT = bv(A_HT, 16384).rearrange("p (a t) -> p a t", a=8)
    qT = bv(A_R2, 8192).rearrange("p (a t) -> p a t", a=4)
    kT = bv(A_R2 + 4096, 8192).rearrange("p (a t) -> p a t", a=4)
    vtile = bv(A_R2 + 8192, 1024).rearrange("p (a t) -> p a t", a=2)
    UW = 2080
    uT = bv(A_R2, 4 * UW).rearrange("p (a t) -> p a t", a=4)
    diag01 = bv(A_R2 + 4160, 2 * 31 * 128).rearrange("p (c j m) -> p c j m", c=2, j=31)
    diag23 = bv(A_R3 + 4096, 2 * 31 * 128).rearrange("p (c j m) -> p c j m", c=2, j=31)
    zT = bv(A_R2, 16384).rearrange("p (a t) -> p a t", a=8)
    aT = bv(A_R2, 16384).rearrange("p (a t) -> p a t", a=8)
    ctab = fv(A_R3, 2048)
    stab = fv(A_R3 + 2048, 2048)
    xst1 = fv(A_R3 + 4096, 4096).rearrange("p (a t) -> p a t", a=2)
    ucT = bv(A_R3, 8192).rearrange("p (a t) -> p a t", a=4)
    attnT = bv(A_R3 + 4096, 8192).rearrange("p (a t) -> p a t", a=4)
    wsl = bv(A_W, 12288).rearrange("p (s k m) -> p s k m", s=12, k=8)
    wv_view = bv(A_W + 4096, 4096).rearrange("p (k n) -> p k n", k=8)
    sb = bv(A_SB, 4096)

    ident = cm[:, M_ID:M_ID + 128]
    bd_ones = cm[:, M_BD:M_BD + 128]
    ones_m = cm[:, M_ONES:M_ONES + 128]
    rotR = cm[:, M_ROT:M_ROT + 128]
    sel = cm[:, M_SEL:M_SEL + 64].rearrange("p (h j) -> p h j", h=8)

    sc = Sched()
    add = sc.add
    outT3 = outT.rearrange("(a p) t -> p a t", p=128)

    def chk(name, ap):
        if stop != name:
            return
        if len(ap.shape) == 2:
            dst = outT[0:ap.shape[0], 0:ap.shape[1]]
        else:
            dst = outT3[0:ap.shape[0], 0:ap.shape[1], 0:ap.shape[2]]
        keys = list(dict.fromkeys(list(sc.last_w.keys()) + list(sc.readers.keys())))
        add("pool", lambda e: e.dma_start(out=dst, in_=ap), reads=keys, writes=[("out", 0)], dma_sem="out")
        raise _Stop()

    try:
        _phases(locals())
    except _Stop:
        pass
    _emit(locals())
    es.close()
    return nc


def _phases(L):
    globals_ = L
    (nc, add, sc, chk, fv, bv, arena, vecs, cm, cf, psb, acc, den, fs, sp_f, x1T, hT, qT, kT, vtile, uT, diag01, diag23, zT, aT,
     ctab, stab, xst1, ucT, attnT, wsl, wv_view, sb, ident, bd_ones, ones_m, rotR, sel, xT, posr, vecs_d, cm_d, w_in_c, w_v,
     w_oa, w_pw, w_out_c, w_fi, w_fo, outT, ps_all) = [L[k] for k in (
        "nc", "add", "sc", "chk", "fv", "bv", "arena", "vecs", "cm", "cf", "psb", "acc", "den", "fs", "sp_f", "x1T", "hT", "qT", "kT",
        "vtile", "uT", "diag01", "diag23", "zT", "aT", "ctab", "stab", "xst1", "ucT", "attnT", "wsl", "wv_view", "sb", "ident",
        "bd_ones", "ones_m", "rotR", "sel", "xT", "posr", "vecs_d", "cm_d", "w_in_c", "w_v", "w_oa", "w_pw", "w_out_c", "w_fi", "w_fo",
        "outT", "ps_all")]

    add("sp", lambda e: e.dma_start(out=vecs, in_=vecs_d), writes=["vecs"], dma_sem="const")
    add("pool", lambda e: e.dma_start(out=cm, in_=cm_d), writes=["cm"], dma_sem="constb")
    add("pool", lambda e: e.memset(cf[:, 0:1], EPS), writes=["cf"])

    def load_w(slot, src, n=1024, after=()):
        dst = bv(A_W + slot * 512, n)
        add("pool", lambda e: e.dma_start(out=dst, in_=src), reads=list(after), writes=[("w", slot + i) for i in range(n // 1024)],
            dma_sem="w%d" % slot)

    def load_qkv(g, after=()):
        for hp in range(4):
            load_w(hp, w_in_c[g * 4 + hp], after=(after if hp > 0 else ()))
        for hp in range(4):
            load_w(4 + hp, w_in_c[12 + g * 4 + hp], after=after)
        load_w(8, w_v[g], n=4096, after=after)

    def barrier():
        keys = list(sc.last_w.keys()) + list(sc.readers.keys())
        keys = list(dict.fromkeys(keys))
        sc.barrier_op = None
        sc.barrier_op = add("pool", lambda e: e.nop(), reads=[], writes=keys)

    def rms_pass1(c, xap, xkey, bank0):
        sqb = sb[:, 0:4096].rearrange("p (a t) -> p a t", a=2)
        add("act", lambda e, xap=xap, c=c: e.activation(out=sqb[:, c % 2, :], in_=xap, func=AF.Square),
            reads=[xkey], writes=[("sq", c % 2)])
        for tb in range(NT):
            add("pe", lambda e, c=c, tb=tb: e.matmul(psb[bank0 + tb], lhsT=ones_m, rhs=sqb[:, c % 2, tb * TB:(tb + 1) * TB],
                                                     start=(c == 0), stop=(c == 7)),
                reads=[("sq", c % 2), "cm"], writes=[("ps", bank0 + tb)])

    def rms_rstd(rstd, bank0):
        for tb in range(NT):
            add("act", lambda e, tb=tb: e.activation(out=rstd[:, tb * TB:(tb + 1) * TB], in_=psb[bank0 + tb], func=AF.Ln,
                                                     bias=cf[:, 0:1], scale=1.0 / D),
                reads=[("ps", bank0 + tb), "cf"], writes=[("rstd", tb)])
            add("act", lambda e, tb=tb: e.activation(out=rstd[:, tb * TB:(tb + 1) * TB], in_=rstd[:, tb * TB:(tb + 1) * TB],
                                                     func=AF.Exp, scale=-0.5),
                reads=[("rstd", tb)], writes=[("rstd", tb)])

    def rms_pass2(c, xap, xkey, dst, wcol, tag, rstd):
        add("dve", lambda e, xap=xap, c=c: e.scalar_tensor_tensor(out=dst[:, c, :], in0=xap, scalar=vecs[:, wcol + c:wcol + c + 1],
                                                                  in1=rstd, op0=ALU.mult, op1=ALU.mult),
            reads=[xkey, "vecs"] + [("rstd", tb) for tb in range(NT)], writes=[(tag, c)])

    def rms_pass2_tb(xaps, xkeys, dst, wcol, tag, rstd):
        for tb in range(NT):
            tok = slice(tb * TB, (tb + 1) * TB)
            for c in range(8):
                add("dve", lambda e, c=c, tok=tok: e.scalar_tensor_tensor(out=dst[:, c, tok], in0=xaps[c][:, tok],
                                                                          scalar=vecs[:, wcol + c:wcol + c + 1], in1=rstd[:, tok],
                                                                          op0=ALU.mult, op1=ALU.mult),
                    reads=[xkeys[c], "vecs", ("rstd", tb)], writes=[(tag, c, tb)])

    def rms_phase(load_chunk, dst, wcol, stage_key, tag, rstd, bank0=0):
        for c in range(8):
            xap, xkey = load_chunk(c, 0)
            rms_pass1(c, xap, xkey, bank0)
        rms_rstd(rstd, bank0)
        for c in range(8):
            xap, xkey = load_chunk(c, 1)
            rms_pass2(c, xap, xkey, dst, wcol, tag, rstd)

    xall = [fv(A_ACC + c * 2048, 2048) if c < 4 else fv(A_R2 + (c - 4) * 2048, 2048) for c in range(8)]
    posi = fv(A_FS, 2048).bitcast(I32)
    angk = fv(A_FS + 2048, 2048)
    ang = sp_f
    for c in range(8):
        add("sp", lambda e, c=c: e.dma_start(out=xall[c], in_=xT[c * 128:(c + 1) * 128, :]),
            writes=[("xall", c)], dma_sem="xa%d" % c)
        if c == 1:
            add("sp", lambda e: e.dma_start(out=posi, in_=posr), writes=["fs0"], dma_sem="xst0")
    load_qkv(0, after=[("xall", 7)])
    for c in range(8):
        rms_pass1(c, xall[c], ("xall", c), 0)
    rstd1 = fv(A_DEN, 2048)
    rms_rstd(rstd1, 0)
    add("dve", lambda e: e.tensor_copy(out=angk, in_=posi), reads=["fs0"], writes=["fs1"])
    add("dve", lambda e: e.tensor_scalar(out=ang, in0=angk, scalar1=vecs[:, V_INVF:V_INVF + 1], scalar2=None,
                                         op0=ALU.mult), reads=["fs1", "vecs"], writes=["ang"])
    add("dve", lambda e: e.scalar_tensor_tensor(out=ang, in0=angk, scalar=vecs[:, V_INVF + 1:V_INVF + 2], in1=ang,
                                                op0=ALU.mult, op1=ALU.add), reads=["fs1", "ang", "vecs"], writes=["ang"])
    add("dve", lambda e: e.tensor_scalar(out=angk, in0=ang, scalar1=float(1.0 / (2 * np.pi)), scalar2=None,
                                         op0=ALU.mult), reads=["ang"], writes=["fs1"])
    add("dve", lambda e: e.tensor_copy(out=posi, in_=angk), reads=["fs1"], writes=["fs0"])
    add("dve", lambda e: e.tensor_copy(out=angk, in_=posi), reads=["fs0"], writes=["fs1"])
    C1 = 6.28125
    C2 = float(2 * np.pi - 6.28125)
    add("dve", lambda e: e.scalar_tensor_tensor(out=ang, in0=angk, scalar=-C1, in1=ang, op0=ALU.mult, op1=ALU.add),
        reads=["fs1", "ang"], writes=["ang"])
    add("dve", lambda e: e.scalar_tensor_tensor(out=ang, in0=angk, scalar=-C2, in1=ang, op0=ALU.mult, op1=ALU.add),
        reads=["fs1", "ang"], writes=["ang"])
    chk("P0a", ang)
    add("dve", lambda e: e.tensor_scalar(out=angk, in0=ang, scalar1=-PI, scalar2=PI, op0=ALU.max, op1=ALU.min),
        reads=["ang"], writes=["fs1"])
    chk("P0k", angk)
    add("act", lambda e: e.activation(out=stab, in_=angk, func=AF.Sin), reads=["fs1"], writes=["stab"])
    r2 = fv(A_FS, 2048)
    add("dve", lambda e: e.tensor_scalar(out=ang, in0=ang, scalar1=PI / 2, scalar2=None, op0=ALU.add),
        reads=["ang"], writes=["ang"])
    add("dve", lambda e: e.tensor_scalar(out=r2, in0=ang, scalar1=PI, scalar2=-2 * PI, op0=ALU.is_gt, op1=ALU.mult),
        reads=["ang"], writes=["fs0"])
    add("dve", lambda e: e.tensor_tensor(out=ang, in0=ang, in1=r2, op=ALU.add), reads=["ang", "fs0"], writes=["ang"])
    add("dve", lambda e: e.tensor_scalar(out=ang, in0=ang, scalar1=-PI, scalar2=PI, op0=ALU.max, op1=ALU.min),
        reads=["ang"], writes=["ang"])
    add("act", lambda e: e.activation(out=ctab, in_=ang, func=AF.Sin), reads=["ang"], writes=["ctab"])
    chk("P0s", stab)
    chk("P0c", ctab)
    rms_pass2_tb(xall, [("xall", c) for c in range(8)], hT, V_N1W, "hT", rstd1)
    XALL_HI = [("xall", c) for c in range(4, 8)]
    chk("P1", hT)

    HT_ALL = [("hT", c) for c in range(8)]


    qw_ap = vecs[:, V_QW:V_QW + 1]
    kw_ap = vecs[:, V_KW:V_KW + 1]
    sbk = sb.rearrange("p (a t) -> p a t", a=8)
    fsk = fs.rearrange("p (a t) -> p a t", a=8)
    pT2 = sb[:, 3072:4096].rearrange("p (a t) -> p a t", a=2)
    QA, QBk, QC = (0, 1, 2, 7), (3, 4), (5, 6)

    def qk_units(g):
        units = []
        for j in range(8):
            for tb in range(NT):
                units.append((j, tb))
        n = len(units)

        def bufs(i):
            return dict(A=QA[i % 4], B=QBk[i % 2], C=QC[i % 2], sq=i % 3, qn=3 + i % 3, rs=i % 3, t1=3 + i % 2, t2=5 + i % 2)

        def e_proj(i):
            j, tb = units[i]
            b = bufs(i)
            A = psb[b["A"]]
            for kc in range(8):
                add("pe", lambda e, j=j, kc=kc, tb=tb, A=A: e.matmul(A, lhsT=wsl[:, j, kc, :], rhs=hT[:, kc, tb * TB:(tb + 1) * TB],
                                                                     start=(kc == 0), stop=(kc == 7)),
                    reads=[("w", j), ("hT", kc, tb)], writes=[("ps", b["A"])])

        def e_sq(i):
            b = bufs(i)
            A = psb[b["A"]]
            add("act", lambda e, A=A, b=b: e.activation(out=sbk[:, b["sq"], :], in_=A, func=AF.Square),
                reads=[("ps", b["A"])], writes=[("sbk", b["sq"])])

        def e_ss(i):
            b = bufs(i)
            B = psb[b["B"]]
            add("pe", lambda e, B=B, b=b: e.matmul(B, lhsT=bd_ones, rhs=sbk[:, b["sq"], :], start=True, stop=True),
                reads=[("sbk", b["sq"]), "cm"], writes=[("ps", b["B"])])

        def e_sqrt(i):
            b = bufs(i)
            B = psb[b["B"]]
            add("act", lambda e, B=B, b=b: e.activation(out=fsk[:, b["rs"], :], in_=B, func=AF.Ln, bias=cf[:, 0:1], scale=1.0 / 64),
                reads=[("ps", b["B"]), "cf"], writes=[("fsk", b["rs"])])

        def e_recip(i):
            b = bufs(i)
            add("act", lambda e, b=b: e.activation(out=fsk[:, b["rs"], :], in_=fsk[:, b["rs"], :], func=AF.Exp, scale=-0.5),
                reads=[("fsk", b["rs"])], writes=[("fsk", b["rs"])])

        def e_qn(i):
            j, tb = units[i]
            b = bufs(i)
            A = psb[b["A"]]
            wap = qw_ap if j < 4 else kw_ap
            add("dve", lambda e, A=A, b=b, wap=wap: e.scalar_tensor_tensor(out=sbk[:, b["qn"], :], in0=A, scalar=wap, in1=fsk[:, b["rs"], :],
                                                                          op0=ALU.mult, op1=ALU.mult),
                reads=[("ps", b["A"]), ("fsk", b["rs"]), "vecs"], writes=[("sbk", b["qn"])])

        def e_rq(i):
            b = bufs(i)
            Cb = psb[b["C"]]
            add("pe", lambda e, Cb=Cb, b=b: e.matmul(Cb, lhsT=rotR, rhs=sbk[:, b["qn"], :], start=True, stop=True),
                reads=[("sbk", b["qn"]), "cm"], writes=[("ps", b["C"])])

        def e_t1(i):
            j, tb = units[i]
            b = bufs(i)
            add("pool", lambda e, b=b, tb=tb: e.tensor_tensor(out=fsk[:, b["t1"], :], in0=sbk[:, b["qn"], :], in1=ctab[:, tb * TB:(tb + 1) * TB],
                                                              op=ALU.mult),
                reads=[("sbk", b["qn"]), "ctab"], writes=[("fsk", b["t1"])])

        def e_t2(i):
            j, tb = units[i]
            b = bufs(i)
            Cb = psb[b["C"]]
            add("dve", lambda e, Cb=Cb, b=b, tb=tb: e.tensor_tensor(out=fsk[:, b["t2"], :], in0=Cb, in1=stab[:, tb * TB:(tb + 1) * TB],
                                                                    op=ALU.mult),
                reads=[("ps", b["C"]), "stab"], writes=[("fsk", b["t2"])])

        def e_add(i):
            j, tb = units[i]
            b = bufs(i)
            dd = (1, 4, 16)[g]
            T = (qT if j < 4 else kT)
            if dd == 1:
                dst = T[:, j % 4, tb * TB:(tb + 1) * TB]
                s1v, s2v = fsk[:, b["t1"], :], fsk[:, b["t2"], :]
            else:
                w = TB // dd
                dst = T[:, j % 4, :].rearrange("p (r m) -> p r m", r=dd)[:, :, tb * w:(tb + 1) * w]
                s1v = fsk[:, b["t1"], :].rearrange("p (m r) -> p r m", r=dd)
                s2v = fsk[:, b["t2"], :].rearrange("p (m r) -> p r m", r=dd)
            dkey = ("qT" if j < 4 else "kT", j % 4)
            add("dve" if i % 2 == 0 else "pool", lambda e, dst=dst, s1v=s1v, s2v=s2v: e.tensor_tensor(out=dst, in0=s1v, in1=s2v, op=ALU.add),
                reads=[("fsk", b["t1"]), ("fsk", b["t2"])], writes=[dkey])

        def ok(k):
            return 0 <= k < n

        for i in range(n + 6):
            if ok(i - 2):
                e_ss(i - 2)
            if ok(i - 4):
                e_rq(i - 4)
            if ok(i):
                e_proj(i)
            if ok(i - 2):
                e_sqrt(i - 2)
            if ok(i - 1):
                e_sq(i - 1)
            if ok(i - 2):
                e_recip(i - 2)
            if ok(i - 4):
                e_t2(i - 4)
                e_t1(i - 4)
            if ok(i - 3):
                e_qn(i - 3)
            if ok(i - 5):
                e_add(i - 5)

    def key_jobs(g):
        d = (1, 4, 16)[g]
        L = S // d
        jobs = []
        if g < 2:
            nqt = L // 128
            for r in range(d):
                for b in range(nqt + 1):
                    if b == 0:
                        jobs.append(dict(k0=r, kstep=d, kp=r * L, mk=64, mask=M_FIRST, segs=[(r, 0, True, False)]))
                    elif b == nqt:
                        jobs.append(dict(k0=(L - 64) * d + r, kstep=d, kp=r * L + L - 64, mk=64, mask=M_LAST,
                                         segs=[(r, nqt - 1, False, True)]))
                    else:
                        jobs.append(dict(k0=(128 * b - 64) * d + r, kstep=d, kp=r * L + 128 * b - 64, mk=128, mask=M_MID,
                                         segs=[(r, b - 1, False, True), (r, b, True, False)]))
        else:
            for r in range(d):
                jobs.append(dict(k0=r, kstep=d, kp=r * L, mk=128, mask=M_G2, segs=[(r, 0, True, True)]))
        return jobs, d

    qt_par = {}
    qt_counter = [0]
    V_BANK, ST_PAIRS, NUM_BANKS, DEN_BANK = 4, (5, 0), (7, 3), 2
    pj_count = [0]
    vt_count = [0]

    def attention(g):
        jobs, d = key_jobs(g)
        import os
        if os.environ.get("ATT_JOBS"):
            jobs = jobs[:int(os.environ["ATT_JOBS"])]
        items = []
        for ji in range(len(jobs)):
            for pr in range(4):
                items.append((ji, pr))
        jinfo = {}

        def vproj(ji):
            job = jobs[ji]
            mk = job["mk"]
            ktok = slice(job["k0"], job["k0"] + (mk - 1) * job["kstep"] + 1, job["kstep"])
            vs = vt_count[0] % 2
            vt_count[0] += 1
            for kc in range(8):
                add("pe", lambda e, kc=kc, ktok=ktok, mk=mk: e.matmul(psb[V_BANK][0:mk, :], lhsT=hT[:, kc, ktok], rhs=wv_view[:, kc, :],
                                                                      start=(kc == 0), stop=(kc == 7)),
                    reads=[("hT", kc, 0), ("hT", kc, 1), ("hT", kc, 2), ("hT", kc, 3), ("w", 8), ("w", 9), ("w", 10), ("w", 11)], writes=[("ps", V_BANK)])
            add("act", lambda e, vs=vs, mk=mk: e.activation(out=vtile[0:mk, vs, :], in_=psb[V_BANK][0:mk, :], func=AF.Copy),
                reads=[("ps", V_BANK)], writes=[("vt", vs)])
            seginfo = []
            for (r, qt, is_start, is_stop) in job["segs"]:
                key = (g, r, qt)
                if is_start:
                    qt_par[key] = qt_counter[0] % 2
                    qt_counter[0] += 1
                par = qt_par[key]
                q0 = (128 * qt) * d + r
                qtok = slice(q0, q0 + 127 * d + 1, d)
                seginfo.append((par, qtok, is_start, is_stop))
            L_ = S // d
            r0_, qt0_ = job["segs"][0][0], job["segs"][0][1]
            qp0 = r0_ * L_ + 128 * qt0_
            kperm = slice(job["kp"], job["kp"] + mk)
            qperm = slice(qp0, qp0 + 128 * len(job["segs"]))
            jinfo[ji] = (mk, ktok, vs, seginfo, kperm, qperm)

        def stage_a(ji, pr, slot):
            mk, ktok, vs, seginfo, kperm, qperm = jinfo[ji]
            nq = 128 * len(seginfo)
            b0 = ST_PAIRS[slot]
            mcol = jobs[ji]["mask"]
            for hh in range(2):
                ST = psb[b0 + hh]
                if mk == 128:
                    add("pe", lambda e, ST=ST, nq=nq, mcol=mcol: e.matmul(ST[:, 0:nq], lhsT=ident, rhs=cm[:, mcol:mcol + nq],
                                                                          start=True, stop=False),
                        reads=["cm"], writes=[("ps", b0 + hh)])
                else:
                    r0 = 64 * hh
                    add("pe", lambda e, ST=ST, nq=nq, mcol=mcol, r0=r0: e.matmul(ST[0:64, 0:nq], lhsT=ident[r0:r0 + 64, r0:r0 + 64],
                                                                                 rhs=cm[r0:r0 + 64, mcol:mcol + nq], start=True, stop=False),
                        reads=["cm"], writes=[("ps", b0 + hh)])
            for hh in range(2):
                ST = psb[b0 + hh]
                add("pe", lambda e, ST=ST, hh=hh, pr=pr, kperm=kperm, qperm=qperm, nq=nq, mk=mk: e.matmul(
                    ST[0:mk, 0:nq], lhsT=kT[64 * hh:64 * hh + 64, pr, kperm], rhs=qT[64 * hh:64 * hh + 64, pr, qperm],
                    start=False, stop=True),
                    reads=[("kT", pr), ("qT", pr)], writes=[("ps", b0 + hh)])
            stp = ps_all[:, b0 * 512:(b0 + 2) * 512].rearrange("p (h n) -> p h n", h=2)
            pv = pT2[:, slot, :].rearrange("p (h n) -> p h n", h=2)
            add("act", lambda e, stp=stp, pv=pv, mk=mk, nq=nq: e.activation(out=pv[0:mk, :, 0:nq], in_=stp[0:mk, :, 0:nq], func=AF.Exp, scale=0.125),
                reads=[("ps", b0), ("ps", b0 + 1)], writes=[("sbk", 6 + slot)])

        def stage_b(ji, pr, slot):
            mk, ktok, vs, seginfo, kperm, qperm = jinfo[ji]
            for si, (par, qtok, is_start, is_stop) in enumerate(seginfo):
                for hh in range(2):
                    h = 2 * pr + hh
                    col = hh * 256 + si * 128
                    nb = NUM_BANKS[par]
                    add("pe", lambda e, nb=nb, hh=hh, pr=pr, h=h, vs=vs, mk=mk, slot=slot, col=col, is_start=is_start, is_stop=is_stop:
                        e.matmul(psb[nb][64 * hh:64 * hh + 64, pr * 128:(pr + 1) * 128], lhsT=vtile[0:mk, vs, h * 64:(h + 1) * 64],
                                 rhs=pT2[0:mk, slot, col:col + 128], start=(is_start and pr == 0), stop=(is_stop and pr == 3),
                                 tile_position=(0, 64 * hh)),
                        reads=[("vt", vs), ("sbk", 6 + slot)], writes=[("ps", nb)])
            for hh in range(2):
                h = 2 * pr + hh
                for si, (par, qtok, is_start, is_stop) in enumerate(seginfo):
                    col = hh * 256 + si * 128
                    add("pe", lambda e, par=par, h=h, mk=mk, slot=slot, col=col, is_start=is_start, is_stop=is_stop:
                        e.matmul(psb[DEN_BANK][32 * par:32 * par + 8, 0:128], lhsT=sel[0:mk, h, :],
                                 rhs=pT2[0:mk, slot, col:col + 128], start=(is_start and h == 0), stop=(is_stop and h == 7),
                                 tile_position=(0, 32 * par)),
                        reads=["cm", ("sbk", 6 + slot)], writes=[("psden", par)])
            if pr == 3:
                for (par, qtok, is_start, is_stop) in seginfo:
                    if is_stop:
                        nb = NUM_BANKS[par]
                        add("dve", lambda e, nb=nb, qtok=qtok: e.tensor_tensor(out=acc[:, :, qtok],
                                                                               in0=psb[nb].rearrange("p (a t) -> p a t", a=4),
                                                                               in1=acc[:, :, qtok], op=ALU.add),
                            reads=[("ps", nb), "acc"], writes=["acc"])
                        add("dve", lambda e, par=par, qtok=qtok: e.tensor_tensor(out=den[32 * par:32 * par + 8, qtok],
                                                                                 in0=psb[DEN_BANK][32 * par:32 * par + 8, 0:128],
                                                                                 in1=den[32 * par:32 * par + 8, qtok], op=ALU.add),
                            reads=[("psden", par), "den"], writes=["den"])

        n = len(items)
        for i in range(n + 1):
            if i < n:
                ji, pr = items[i]
                if pr == 0:
                    vproj(ji)
                stage_a(ji, pr, i % 2)
            if i >= 1:
                ji, pr = items[i - 1]
                stage_b(ji, pr, (i - 1) % 2)

    for g in range(3):
        if g > 0:
            load_qkv(g)
        qk_units(g)
        if g == 0:
            add("pool", lambda e: e.memset(acc, 0.0), writes=["acc"] + [("xall", c) for c in range(4)])
            add("pool", lambda e: e.memset(den, 0.0), writes=["den"] + [("rstd", tb) for tb in range(NT)])
            chk("Q0", qT)
            chk("K0", kT)
        attention(g)
        if g == 0:
            chk("A0", acc)
            chk("D0", den)
    chk("A2", acc)
    chk("D2", den)

    for cc in range(4):
        load_w(cc, w_in_c[36 + cc])
        load_w(4 + cc, w_in_c[40 + cc])
    UW_ = 2080
    R2_OLD = [("qT", i_) for i_ in range(4)] + [("kT", i_) for i_ in range(4)] + [("vt", 0), ("vt", 1)]
    add("pool", lambda e: e.memset(uT[:, :, 0:15], 0.0), writes=["uT"] + R2_OLD)
    add("pool", lambda e: e.memset(uT[:, :, 15 + S:UW_], 0.0), writes=["uT"])
    def build_diag(cc, dg):
        idb = ident.rearrange("p (o m) -> p o m", o=1).broadcast_to([128, 31, 128])
        cwb = vecs[:, V_CW + cc * 31:V_CW + cc * 31 + 31].rearrange("p (j o) -> p j o", o=1).broadcast_to([128, 31, 128])
        add("dve", lambda e: e.tensor_tensor(out=dg[:, cc % 2, :, :], in0=idb, in1=cwb, op=ALU.mult),
            reads=["cm", "vecs"], writes=[("diag", cc)] + (["acc"] if cc >= 2 else R2_OLD))


    selb = vecs[0:40, V_SELB:V_SELB + 512].rearrange("p (a m) -> p a m", a=4)
    rdb = sp_f.rearrange("p (a t) -> p a t", a=4)
    p4 = [(pr, tb) for pr in range(4) for tb in range(NT)]

    def p4_mm(k):
        pr, tb = p4[k]
        bank = 4 + k % 4
        add("pe", lambda e: e.matmul(psb[bank], lhsT=selb[:, pr, :], rhs=den[0:40, tb * TB:(tb + 1) * TB], start=True, stop=True),
            reads=["den", "vecs"], writes=[("ps", bank)])

    def p4_ln(k):
        bank = 4 + k % 4
        add("act", lambda e: e.activation(out=rdb[:, k % 4, :], in_=psb[bank], func=AF.Ln),
            reads=[("ps", bank)], writes=[("rdb", k % 4)])

    def p4_exp(k):
        add("act", lambda e: e.activation(out=rdb[:, k % 4, :], in_=rdb[:, k % 4, :], func=AF.Exp, scale=-1.0),
            reads=[("rdb", k % 4)], writes=[("rdb", k % 4)])

    def p4_mul(k):
        pr, tb = p4[k]
        add("dve", lambda e: e.tensor_tensor(out=attnT[:, pr, tb * TB:(tb + 1) * TB], in0=rdb[:, k % 4, :],
                                             in1=acc[:, pr, tb * TB:(tb + 1) * TB], op=ALU.mult),
            reads=[("rdb", k % 4), "acc"], writes=[("attnT", pr)])

    def p4_step(k):
        if k < len(p4):
            p4_mm(k)
        if 0 <= k - 1 < len(p4):
            p4_exp(k - 1)
        if k < len(p4):
            p4_ln(k)
        if 0 <= k - 2 < len(p4):
            p4_mul(k - 2)

    u = 0
    rs2 = fs[:, 0:1024].rearrange("p (a t) -> p a t", a=2)
    for cc in range(4):
        for tb in range(NT):
            par = u % 2
            u += 1
            A, B = psb[par], psb[2 + par]
            for kc in range(8):
                add("pe", lambda e, cc=cc, kc=kc, tb=tb, A=A: e.matmul(A, lhsT=wsl[:, cc, kc, :], rhs=hT[:, kc, tb * TB:(tb + 1) * TB],
                                                                       start=(kc == 0), stop=(kc == 7)),
                    reads=[("w", cc), ("hT", kc, tb)], writes=[("ps", par)])
            for kc in range(8):
                add("pe", lambda e, cc=cc, kc=kc, tb=tb, B=B: e.matmul(B, lhsT=wsl[:, 4 + cc, kc, :], rhs=hT[:, kc, tb * TB:(tb + 1) * TB],
                                                                       start=(kc == 0), stop=(kc == 7)),
                    reads=[("w", 4 + cc), ("hT", kc, tb)], writes=[("ps", 2 + par)])
            add("act", lambda e, B=B, par=par: e.activation(out=rs2[:, par, :], in_=B, func=AF.Sigmoid),
                reads=[("ps", 2 + par)], writes=[("rs2", par)])
            add("dve", lambda e, A=A, par=par, cc=cc, tb=tb: e.tensor_tensor(out=uT[:, cc, 15 + tb * TB:15 + (tb + 1) * TB], in0=A,
                                                                             in1=rs2[:, par, :], op=ALU.mult),
                reads=[("ps", par), ("rs2", par)], writes=["uT"])
            if u == 3:
                build_diag(0, diag01)
            if u == 6:
                build_diag(1, diag01)
            if u % 4 == 0:
                for k_ in range(u - 4, u):
                    p4_step(k_)
    p4_step(16)
    p4_step(17)
    chk("P4", attnT)
    diag23 = bv(A_ACC, 2 * 31 * 128).rearrange("p (c j m) -> p c j m", c=2, j=31)
    for cc in (2, 3):
        build_diag(cc, diag23)
    def load_p5(c):
        if c >= 8:
            return
        s0 = (c % 4) * 3
        oa_v = bv(A_W + s0 * 512, 512).rearrange("p (k m) -> p k m", k=4)
        pw_v = bv(A_W + s0 * 512 + 256, 512).rearrange("p (k m) -> p k m", k=4)
        add("pool", lambda e, oa_v=oa_v, c=c: e.dma_start(out=oa_v, in_=w_oa[c].rearrange("p (k m) -> p k m", k=4)),
            writes=[("w", s0)], dma_sem="w%d" % s0)
        add("pool", lambda e, pw_v=pw_v, c=c: e.dma_start(out=pw_v, in_=w_pw[c].rearrange("p (k m) -> p k m", k=4)),
            writes=[("w", s0)], dma_sem="w%d" % s0)
        load_w(s0 + 1, w_in_c[44 + c])
        load_w(s0 + 2, w_in_c[52 + c])

    load_p5(0)
    load_p5(1)
    load_p5(2)
    ycv2 = fs.rearrange("p (b a t) -> p b a t", b=2, a=4)
    ysq = sb[:, 0:2048].rearrange("p (a t) -> p a t", a=4)
    ybf = sb[:, 2048:4096].rearrange("p (a t) -> p a t", a=4)
    mean2 = sp_f[:, 0:1024].rearrange("p (a t) -> p a t", a=2)
    rstd2 = sp_f[:, 1024:2048].rearrange("p (a t) -> p a t", a=2)

    def conv_cc(tb, cc):
        yb = tb % 2
        dg = diag01 if cc < 2 else diag23
        bank = cc % 2
        sbank = 2 + 2 * yb
        for j in range(31):
            add("pe", lambda e, dg=dg, cc=cc, j=j, tb=tb, bank=bank: e.matmul(psb[bank], lhsT=dg[:, cc % 2, j, :],
                                                                              rhs=uT[:, cc, tb * TB + j:tb * TB + j + TB],
                                                                              start=(j == 0), stop=(j == 30)),
                reads=[("diag", cc), "uT"], writes=[("ps", bank)])
        add("act", lambda e, cc=cc, bank=bank, yb=yb: e.activation(out=ycv2[:, yb, cc, :], in_=psb[bank], func=AF.Identity,
                                                                   bias=vecs[:, V_CB + cc:V_CB + cc + 1], scale=1.0),
            reads=[("ps", bank), "vecs"], writes=[("ycv", yb, cc)])
        add("act", lambda e, cc=cc, yb=yb: e.activation(out=ysq[:, cc, :], in_=ycv2[:, yb, cc, :], func=AF.Square),
            reads=[("ycv", yb, cc)], writes=[("ysq", cc)])
        add("pool", lambda e, cc=cc, yb=yb: e.tensor_copy(out=ybf[:, cc, :], in_=ycv2[:, yb, cc, :]),
            reads=[("ycv", yb, cc)], writes=[("ybf", cc)])

    def conv_stats(tb, cc):
        yb = tb % 2
        sbank = 2 + 2 * yb
        add("pe", lambda e, cc=cc, sbank=sbank: e.matmul(psb[sbank], lhsT=ones_m, rhs=ybf[:, cc, :], start=(cc == 0), stop=(cc == 3)),
            reads=[("ybf", cc), "cm"], writes=[("ps", sbank)])
        add("pe", lambda e, cc=cc, sbank=sbank: e.matmul(psb[sbank + 1], lhsT=ones_m, rhs=ysq[:, cc, :], start=(cc == 0), stop=(cc == 3)),
            reads=[("ysq", cc), "cm"], writes=[("ps", sbank + 1)])

    def ln_stats(tb):
        yb = tb % 2
        sbank = 2 + 2 * yb
        mean_t, var_t = mean2[:, yb, :], rstd2[:, yb, :]
        add("dve", lambda e: e.tensor_scalar(out=mean_t, in0=psb[sbank], scalar1=1.0 / 512, scalar2=None, op0=ALU.mult),
            reads=[("ps", sbank)], writes=[("mean", yb)])
        add("dve", lambda e: e.tensor_tensor(out=var_t, in0=mean_t, in1=mean_t, op=ALU.mult), reads=[("mean", yb)], writes=[("var", yb)])
        add("dve", lambda e: e.scalar_tensor_tensor(out=var_t, in0=psb[sbank + 1], scalar=1.0 / 512, in1=var_t, op0=ALU.mult, op1=ALU.subtract),
            reads=[("ps", sbank + 1), ("var", yb)], writes=[("var", yb)])
        add("act", lambda e: e.activation(out=var_t, in_=var_t, func=AF.Ln, bias=cf[:, 0:1], scale=1.0),
            reads=[("var", yb), "cf"], writes=[("var", yb)])
        add("act", lambda e: e.activation(out=var_t, in_=var_t, func=AF.Exp, scale=-0.5), reads=[("var", yb)], writes=[("var", yb)])

    def ln_tail(tb, cc):
        yb = tb % 2
        y = ycv2[:, yb, cc, :]
        add("dve", lambda e: e.tensor_tensor(out=y, in0=y, in1=mean2[:, yb, :], op=ALU.subtract),
            reads=[("ycv", yb, cc), ("mean", yb)], writes=[("ycv", yb, cc)])
        add("pool", lambda e: e.tensor_tensor(out=y, in0=y, in1=rstd2[:, yb, :], op=ALU.mult),
            reads=[("ycv", yb, cc), ("var", yb)], writes=[("ycv", yb, cc)])
        add("act", lambda e: e.activation(out=ucT[:, cc, tb * TB:(tb + 1) * TB], in_=y, func=AF.Silu,
                                          bias=vecs[:, V_LB + cc:V_LB + cc + 1], scale=vecs[:, V_LW + cc:V_LW + cc + 1]),
            reads=[("ycv", yb, cc), "vecs"], writes=[("ucT", cc, tb)])

    steps = [(tb, cc) for tb in range(NT) for cc in range(4)]
    for k in range(len(steps) + 6):
        if k - 6 >= 0 and k - 6 < len(steps):
            ln_tail(*steps[k - 6])
        if k < len(steps):
            conv_cc(*steps[k])
        if 0 <= k - 1 < len(steps):
            tb1, cc1 = steps[k - 1]
            conv_stats(tb1, cc1)
            if cc1 == 3:
                ln_stats(tb1)

    chk("P3", ucT)

    ga_t = fs[:, 0:1024].rearrange("p (a t) -> p a t", a=2)
    gb_t = fs[:, 1024:2048].rearrange("p (a t) -> p a t", a=2)
    ta_t = fv(A_ACC + 4096, 1024).rearrange("p (a t) -> p a t", a=2)
    tb_t = fv(A_ACC + 5120, 1024).rearrange("p (a t) -> p a t", a=2)
    u = 0
    for c in range(8):
        s0 = (c % 4) * 3
        oa_v = bv(A_W + s0 * 512, 512).rearrange("p (k m) -> p k m", k=4)
        pw_v = bv(A_W + s0 * 512 + 256, 512).rearrange("p (k m) -> p k m", k=4)
        load_p5(c + 3)
        if c == 6:
            for c6 in range(3):
                load_w(c6, w_out_c[c6])
        if c == 7:
            load_w(3, w_out_c[3])
        for tb in range(NT):
            par = u % 2
            u += 1
            YA, YB, GA, GB = psb[4 * par], psb[4 * par + 1], psb[4 * par + 2], psb[4 * par + 3]
            tok = slice(tb * TB, (tb + 1) * TB)
            for kc in range(4):
                add("pe", lambda e, kc=kc, tok=tok, YA=YA, oa_v=oa_v: e.matmul(YA, lhsT=oa_v[:, kc, :], rhs=attnT[:, kc, tok],
                                                                               start=(kc == 0), stop=(kc == 3)),
                    reads=[("w", s0), ("attnT", kc)], writes=[("ps", 4 * par)])
            for kc in range(4):
                add("pe", lambda e, kc=kc, tok=tok, YB=YB, pw_v=pw_v: e.matmul(YB, lhsT=pw_v[:, kc, :], rhs=ucT[:, kc, tok],
                                                                               start=(kc == 0), stop=(kc == 3)),
                    reads=[("w", s0), ("ucT", kc, tb)], writes=[("ps", 4 * par + 1)])
            for kc in range(8):
                add("pe", lambda e, kc=kc, tok=tok, GA=GA, s0=s0: e.matmul(GA, lhsT=wsl[:, s0 + 1, kc, :], rhs=hT[:, kc, tok],
                                                                           start=(kc == 0), stop=(kc == 7)),
                    reads=[("w", s0 + 1), ("hT", kc, tb)], writes=[("ps", 4 * par + 2)])
            for kc in range(8):
                add("pe", lambda e, kc=kc, tok=tok, GB=GB, s0=s0: e.matmul(GB, lhsT=wsl[:, s0 + 2, kc, :], rhs=hT[:, kc, tok],
                                                                           start=(kc == 0), stop=(kc == 7)),
                    reads=[("w", s0 + 2), ("hT", kc, tb)], writes=[("ps", 4 * par + 3)])
            add("act", lambda e, GA=GA, par=par, c=c: e.activation(out=ga_t[:, par, :], in_=GA, func=AF.Sigmoid,
                                                                   bias=vecs[:, V_BGA + c:V_BGA + c + 1], scale=1.0),
                reads=[("ps", 4 * par + 2), "vecs"], writes=[("ga", par), ("ycv", 0, 0), ("ycv", 0, 1)])
            add("act", lambda e, GB=GB, par=par, c=c: e.activation(out=gb_t[:, par, :], in_=GB, func=AF.Sigmoid,
                                                                   bias=vecs[:, V_BGB + c:V_BGB + c + 1], scale=1.0),
                reads=[("ps", 4 * par + 3), "vecs"], writes=[("gb", par), ("ycv", 0, 2), ("ycv", 0, 3)])
            add("dve", lambda e, YA=YA, par=par: e.tensor_tensor(out=ta_t[:, par, :], in0=YA, in1=ga_t[:, par, :], op=ALU.mult),
                reads=[("ps", 4 * par), ("ga", par)], writes=[("ta", par)])
            add("dve", lambda e, YB=YB, par=par: e.tensor_tensor(out=tb_t[:, par, :], in0=YB, in1=gb_t[:, par, :], op=ALU.mult),
                reads=[("ps", 4 * par + 1), ("gb", par)], writes=[("tb", par)])
            add("pool", lambda e, par=par, c=c, tok=tok: e.tensor_tensor(out=zT[:, c, tok], in0=ta_t[:, par, :], in1=tb_t[:, par, :], op=ALU.add),
                reads=[("ta", par), ("tb", par)], writes=[("zT", c), "uT", ("diag", 0), ("diag", 1)])

    chk("P5", zT)
    barrier()
    xst6 = fv(A_R3, 4096).rearrange("p (a t) -> p a t", a=2)
    rstd7 = fv(A_R3 + 4096, 2048)
    u = 0
    for c in range(9):
        if c < 8:
            slot = c % 4
            if c >= 4:
                load_w(slot, w_out_c[c])
            xs = c % 2
            add("sp", lambda e, c=c, xs=xs: e.dma_start(out=xst6[:, xs, :], in_=xT[c * 128:(c + 1) * 128, :]),
                writes=[("xst6", xs)], dma_sem="xst%d" % xs)
            for tb in range(NT):
                bank = u % 4
                u += 1
                tok = slice(tb * TB, (tb + 1) * TB)
                for kc in range(8):
                    add("pe", lambda e, kc=kc, tok=tok, bank=bank, slot=slot: e.matmul(psb[bank], lhsT=wsl[:, slot, kc, :], rhs=zT[:, kc, tok],
                                                                                       start=(kc == 0), stop=(kc == 7)),
                        reads=[("w", slot), ("zT", kc)], writes=[("ps", bank)])
                add("dve", lambda e, c=c, tok=tok, bank=bank, xs=xs: e.tensor_tensor(out=x1T[:, c, tok], in0=psb[bank], in1=xst6[:, xs, tok], op=ALU.add),
                    reads=[("ps", bank), ("xst6", xs)], writes=[("x1T", c)])
        if c >= 1:
            rms_pass1(c - 1, x1T[:, c - 1, :], ("x1T", c - 1), 4)
    chk("P6", x1T)
    rms_rstd(rstd7, 4)
    rms_pass2_tb([x1T[:, c, :] for c in range(8)], [("x1T", c) for c in range(8)], hT, V_N2W, "h2T", rstd7)

    chk("P7", hT)
    sg_t = fv(A_R3 + 6144, 1024).rearrange("p (a t) -> p a t", a=2)
    groups = [list(range(0, 8)), list(range(8, 16)), list(range(16, 22))]
    u = 0
    fcount = 0
    out_ops = []
    all_f = [f for grp in groups for f in grp]

    def load_fi(k):
        if k < len(all_f):
            sgk = 8 + 2 * (k % 2)
            load_w(sgk, w_fi[all_f[k]])
            load_w(sgk + 1, w_fi[22 + all_f[k]])

    load_fi(0)
    for gi, grp in enumerate(groups):
        for fi, f in enumerate(grp):
            sg_ = 8 + 2 * (fcount % 2)
            fcount += 1
            load_fi(fcount)
            if fi == 1:
                for fj, f2 in enumerate(grp):
                    load_w(fj, w_fo[f2])
            for tb in range(NT):
                par = u % 2
                u += 1
                tok = slice(tb * TB, (tb + 1) * TB)
                GT, UP = psb[2 * par], psb[2 * par + 1]
                for kc in range(8):
                    add("pe", lambda e, kc=kc, tok=tok, GT=GT, sg_=sg_: e.matmul(GT, lhsT=wsl[:, sg_, kc, :], rhs=hT[:, kc, tok],
                                                                                 start=(kc == 0), stop=(kc == 7)),
                        reads=[("w", sg_), ("h2T", kc, tb)], writes=[("ps", 2 * par)])
                for kc in range(8):
                    add("pe", lambda e, kc=kc, tok=tok, UP=UP, sg_=sg_: e.matmul(UP, lhsT=wsl[:, sg_ + 1, kc, :], rhs=hT[:, kc, tok],
                                                                                 start=(kc == 0), stop=(kc == 7)),
                        reads=[("w", sg_ + 1), ("h2T", kc, tb)], writes=[("ps", 2 * par + 1)])
                add("act", lambda e, GT=GT, par=par: e.activation(out=sg_t[:, par, :], in_=GT, func=AF.Silu),
                    reads=[("ps", 2 * par)], writes=[("sg", par)])
                add("dve", lambda e, UP=UP, par=par, fi=fi, tok=tok: e.tensor_tensor(out=aT[:, fi, tok], in0=UP, in1=sg_t[:, par, :], op=ALU.mult),
                    reads=[("ps", 2 * par + 1), ("sg", par)], writes=[("aT", fi)])
        last = gi == len(groups) - 1
        for c in range(8):
            for tb in range(NT):
                bank = 4 + (u % 4)
                u += 1
                tok = slice(tb * TB, (tb + 1) * TB)
                for fi in range(len(grp)):
                    add("pe", lambda e, fi=fi, c=c, tok=tok, bank=bank, n=len(grp): e.matmul(
                        psb[bank], lhsT=bv(A_W + fi * 512, 1024)[:, c * 128:(c + 1) * 128], rhs=aT[:, fi, tok],
                        start=(fi == 0), stop=(fi == n - 1)),
                        reads=[("w", fi), ("aT", fi)], writes=[("ps", bank)])
                add("dve", lambda e, c=c, tok=tok, bank=bank: e.tensor_tensor(out=x1T[:, c, tok], in0=psb[bank], in1=x1T[:, c, tok], op=ALU.add),
                    reads=[("ps", bank), ("x1T", c)], writes=[("x1T", c)] + ([("x1o", c, tb)] if last else []))
                if last:
                    out_ops.append(add("sp", lambda e, c=c, tok=tok: e.dma_start(out=outT[c * 128:(c + 1) * 128, tok], in_=x1T[:, c, tok]),
                                       reads=[("x1o", c, tb)], writes=[("out", c, tb)], dma_sem="out"))


def _emit(L):
    nc, sc, es = L["nc"], L["sc"], L["es"]
    sc.finalize()
    sem_names = ["pe", "act", "dve", "pool", "sp"] + sorted({op.sem_key for op in sc.ops if op.is_dma})
    sems = {n: es.enter_context(nc.semaphore("s_" + n)) for n in sem_names}
    out_total = sc.totals.get("out", 0)

    def emitter(engname):
        def f(e):
            for op in sc.ops:
                if op.eng != engname:
                    continue
                for (sn, v) in op.waits:
                    e.wait_ge(sems[sn], v)
                if op.fn is None:
                    continue
                ins = op.fn(e)
                if op.is_dma:
                    ins.then_inc(sems[op.sem_key], 16)
                elif op.signal:
                    ins.then_inc(sems[op.eng], 1)
            if engname == "sp":
                e.wait_ge(sems["out"], out_total)
        return f

    with nc.Block() as block:
        block.sync(emitter("sp"))
        block.tensor(emitter("pe"))
        block.scalar(emitter("act"))
        block.vector(emitter("dve"))
        block.gpsimd(emitter("pool"))


def _consts():
    cmh = np.zeros((128, NCM), np.float32)
    cmh[:, M_ID:M_ID + 128] = np.eye(128, dtype=np.float32)
    bd = np.zeros((128, 128), np.float32)
    bd[:64, :64] = 1.0
    bd[64:, 64:] = 1.0
    cmh[:, M_BD:M_BD + 128] = bd
    cmh[:, M_ONES:M_ONES + 128] = 1.0
    R = np.zeros((128, 128), np.float32)
    for base in (0, 64):
        for i in range(8):
            R[base + i + 8, base + i] = -1.0
            R[base + i, base + i + 8] = 1.0
    cmh[:, M_ROT:M_ROT + 128] = R
    i = np.arange(128)[:, None]
    jq = np.arange(256)[None, :]
    mid = np.where((jq >= i) & (jq <= i + 128), 0.0, NEG).astype(np.float32)
    cmh[:, M_MID:M_MID + 256] = mid
    cmh[:, M_MID + 256:M_MID + 512] = mid
    j1 = np.arange(128)[None, :]
    first = np.where(np.abs(j1 - i) <= 64, 0.0, NEG).astype(np.float32)
    first[64:] = first[:64]
    cmh[:, M_FIRST:M_FIRST + 128] = first
    cmh[:, M_FIRST + 128:M_FIRST + 256] = first
    last = np.where(j1 >= i, 0.0, NEG).astype(np.float32)
    last[64:] = last[:64]
    cmh[:, M_LAST:M_LAST + 128] = last
    cmh[:, M_LAST + 128:M_LAST + 256] = last
    g2 = np.where(np.abs(j1 - i) <= 64, 0.0, NEG).astype(np.float32)
    cmh[:, M_G2:M_G2 + 128] = g2
    cmh[:, M_G2 + 128:M_G2 + 256] = g2
    selm = np.zeros((128, 8, 8), np.float32)
    for h in range(8):
        selm[:, h, h] = 1.0
    cmh[:, M_SEL:M_SEL + 64] = selm.reshape(128, 64)
    return cmh


_NC_CACHE = {}


def kernel(x, positions, norm1_w, w_in, b_gate, q_norm_w, k_norm_w, w_o_attn, conv_w, conv_b, conv_ln_w,
           conv_ln_b, w_pw_conv, w_out, norm2_w, w_ffn_in, w_ffn_out, _prep_only=False):
    f32 = np.float32
    x = np.asarray(x, f32)
    positions = np.asarray(positions, np.int32)
    B = x.shape[0]
    l = 0
    vec = np.zeros((128, NV), f32)
    vec[:, V_N1W:V_N1W + 8] = np.asarray(norm1_w, f32)[l].reshape(8, 128).T
    vec[:, V_N2W:V_N2W + 8] = np.asarray(norm2_w, f32)[l].reshape(8, 128).T
    vec[:, V_QW] = np.tile(np.asarray(q_norm_w, f32)[l], 2)
    vec[:, V_KW] = np.tile(np.asarray(k_norm_w, f32)[l], 2)
    bg = np.asarray(b_gate, f32)[l]
    vec[:, V_BGA:V_BGA + 8] = bg[0].reshape(8, 128).T
    vec[:, V_BGB:V_BGB + 8] = bg[1].reshape(8, 128).T
    vec[:, V_CB:V_CB + 4] = np.asarray(conv_b, f32)[l].reshape(4, 128).T
    vec[:, V_LW:V_LW + 4] = np.asarray(conv_ln_w, f32)[l].reshape(4, 128).T
    vec[:, V_LB:V_LB + 4] = np.asarray(conv_ln_b, f32)[l].reshape(4, 128).T
    invf = np.zeros(128, f32)
    invf_lo = np.zeros(128, f32)
    inv_freq64 = 500000.0 ** (-np.arange(0, 16, 2, dtype=np.float64) / 16.0)
    inv_hi = inv_freq64.astype(f32)
    inv_lo = (inv_freq64 - inv_hi.astype(np.float64)).astype(f32)
    for base in (0, 64):
        invf[base:base + 8] = inv_hi
        invf[base + 8:base + 16] = inv_hi
        invf_lo[base:base + 8] = inv_lo
        invf_lo[base + 8:base + 16] = inv_lo
    vec[:, V_INVF] = invf
    vec[:, V_INVF + 1] = invf_lo
    cw = np.asarray(conv_w, f32)[l]
    vec[:, V_CW:V_CW + 124] = cw.T.reshape(4, 128, 31).transpose(1, 0, 2).reshape(128, 124)
    selb = np.zeros((8, 4, 128), f32)
    for pr in range(4):
        selb[2 * pr, pr, :64] = 1.0
        selb[2 * pr + 1, pr, 64:] = 1.0
    vec[0:8, V_SELB:V_SELB + 512] = selb.reshape(8, 512)
    vec[32:40, V_SELB:V_SELB + 512] = selb.reshape(8, 512)
    cmh = _consts()

    def chunk_lhsT(w):
        K, N = w.shape
        return np.ascontiguousarray(w.reshape(K // 128, 128, N // 128, 128).transpose(2, 1, 0, 3)).reshape(N // 128, 128, (K // 128) * 128)

    w_in0 = np.asarray(w_in, f32)[l]
    w_in_c = chunk_lhsT(w_in0)
    w_v = np.ascontiguousarray(w_in0[:, 3072:4608].reshape(8, 128, 3, 512).transpose(2, 1, 0, 3)).reshape(3, 128, 4096)
    w_oa = chunk_lhsT(np.asarray(w_o_attn, f32)[l])
    w_pw = chunk_lhsT(np.asarray(w_pw_conv, f32)[l])
    w_out_c = chunk_lhsT(np.asarray(w_out, f32)[l])
    w_fi = chunk_lhsT(np.asarray(w_ffn_in, f32)[l])
    w_fo = np.ascontiguousarray(np.asarray(w_ffn_out, f32)[l].reshape(22, 128, 1024))

    in_maps = []
    for b in range(B):
        in_maps.append({
            "xT": np.ascontiguousarray(x[b].T),
            "posr": np.ascontiguousarray(np.broadcast_to(positions[b][None, :], (128, S))),
            "vecs": vec, "cmats": cmh, "w_in_c": w_in_c, "w_v": w_v, "w_oa": w_oa, "w_pw": w_pw,
            "w_out_c": w_out_c, "w_fi": w_fi, "w_fo": w_fo,
        })
    if _prep_only:
        return in_maps
    if "nc" not in _NC_CACHE:
        _NC_CACHE["nc"] = build_program()
    nc = _NC_CACHE["nc"]
    res = run_bass_kernel_spmd(nc, in_maps, core_ids=list(range(B)))
    out = np.stack([np.ascontiguousarray(np.asarray(r["outT"], f32).T) for r in res.results], axis=0)
    return out.astype(f32)
```
